# Optimizing a Trainium2 kernel written in Bass

```python
import math
import jax, jax.numpy as jnp
from jax import lax
import numpy as np

D_MODEL = 1024
BATCH = 4
SEQ = 4096
DEPTH = 2

HEAD_DIM = 64
BLOCK = 128
WINDOW = 128
A_Q_HEADS = 8
A_KV_HEADS = 2
A_GROUP = A_Q_HEADS // A_KV_HEADS
B_HEADS = 4
B_V_DIM = 2 * HEAD_DIM
D_FF = 2816
RMS_EPS = 1e-6
NEG_INF = -1e30

A_Q_COLS = A_Q_HEADS * HEAD_DIM
A_KV_COLS = A_KV_HEADS * HEAD_DIM
B_QK_COLS = B_HEADS * 2 * HEAD_DIM
B_V_COLS = B_HEADS * B_V_DIM
IN_COLS = A_Q_COLS + 2 * A_KV_COLS + 2 * B_QK_COLS + B_V_COLS
MIX_WIDTH = A_Q_COLS + B_V_COLS
SPLITS = list(np.cumsum([A_Q_COLS, A_KV_COLS, A_KV_COLS, B_QK_COLS, B_QK_COLS]))

kernel_name = "hybrid_swa_sink_diffattn_alibi_macaron"


def alibi_slopes(n):
    return jnp.exp2(-8.0 * jnp.arange(1, n + 1, dtype=jnp.float32) / n)


def rms_norm(x, g):
    xf = x.astype(jnp.float32)
    y = xf * lax.rsqrt(jnp.mean(xf * xf, axis=-1, keepdims=True) + RMS_EPS)
    return (y * g.astype(jnp.float32)).astype(x.dtype)


def swiglu(h, w_gate, w_up, w_down):
    return (jax.nn.silu(h @ w_gate) * (h @ w_up)) @ w_down


def windowed_gqa_sink(q, k, v, sink, slopes):
    b, s, _, dh = q.shape
    nb = s // BLOCK
    qb = q.reshape(b, nb, BLOCK, A_KV_HEADS, A_GROUP, dh)
    pad = ((0, 0), (BLOCK, BLOCK), (0, 0), (0, 0))
    kp = jnp.pad(k, pad)
    vp = jnp.pad(v, pad)
    key_idx = jnp.arange(nb)[:, None] * BLOCK + jnp.arange(3 * BLOCK)[None, :]
    kb = kp[:, key_idx]
    vb = vp[:, key_idx]
    scores = jnp.einsum('bnqkgd,bnjkd->bnkgqj', qb, kb).astype(jnp.float32) * (dh ** -0.5)
    q_pos = jnp.arange(nb)[:, None] * BLOCK + jnp.arange(BLOCK)[None, :]
    k_pos = key_idx - BLOCK
    dist = jnp.abs(q_pos[:, :, None] - k_pos[:, None, :])
    valid = (dist <= WINDOW) & (k_pos[:, None, :] >= 0) & (k_pos[:, None, :] < s)
    bias = -slopes.reshape(A_KV_HEADS, A_GROUP)[None, :, :, None, None] * dist[:, None, None].astype(jnp.float32)
    scores = jnp.where(valid[:, None, None], scores + bias, NEG_INF)
    sink_l = sink.astype(jnp.float32).reshape(1, 1, A_KV_HEADS, A_GROUP, 1, 1)
    m = jnp.maximum(jnp.max(scores, axis=-1, keepdims=True), sink_l)
    e = jnp.exp(scores - m)
    p = e / (jnp.sum(e, axis=-1, keepdims=True) + jnp.exp(sink_l - m))
    out = jnp.einsum('bnkgqj,bnjkd->bnqkgd', p.astype(v.dtype), vb)
    return out.reshape(b, s, A_Q_HEADS * dh)


def differential_attention(q, k, v, lam, slopes):
    b, s, h, _, dh = q.shape
    nb = s // BLOCK
    q_blocks = jnp.moveaxis(q.reshape(b, nb, BLOCK, h, 2, dh), 1, 0)
    starts = jnp.arange(nb) * BLOCK
    key_pos = jnp.arange(s)
    scale = dh ** -0.5

    def one_block(args):
        qi, start = args
        sc = jnp.einsum('bqhcd,bkhcd->bhcqk', qi, k).astype(jnp.float32) * scale
        dist = jnp.abs((start + jnp.arange(BLOCK))[:, None] - key_pos[None, :]).astype(jnp.float32)
        sc = sc - slopes[:, None, None, None] * dist
        p = jax.nn.softmax(sc, axis=-1)
        a = p[:, :, 0] - lam * p[:, :, 1]
        return jnp.einsum('bhqk,bkhe->bqhe', a.astype(v.dtype), v)

    out = lax.map(one_block, (q_blocks, starts))
    return jnp.moveaxis(out, 0, 1).reshape(b, s, h, -1)


def setup_inputs(seed: int = 0) -> dict:
    key = jax.random.key(seed)
    ks = jax.random.split(key, 24)
    f32 = jnp.float32

    def nrm(k, shape, scale):
        return jax.random.normal(k, shape, f32) * scale

    def gain(k, shape):
        return 1.0 + 0.02 * jax.random.normal(k, shape, f32)

    return {
        "x": jax.random.normal(ks[0], (BATCH, SEQ, D_MODEL), f32),
        "ffn1_norm": gain(ks[1], (DEPTH, D_MODEL)),
        "ffn1_w_gate": nrm(ks[2], (DEPTH, D_MODEL, D_FF), D_MODEL ** -0.5),
        "ffn1_w_up": nrm(ks[3], (DEPTH, D_MODEL, D_FF), D_MODEL ** -0.5),
        "ffn1_w_down": nrm(ks[4], (DEPTH, D_FF, D_MODEL), D_FF ** -0.5),
        "mix_norm": gain(ks[5], (DEPTH, D_MODEL)),
        "w_in": nrm(ks[6], (DEPTH, D_MODEL, IN_COLS), D_MODEL ** -0.5),
        "sink": nrm(ks[7], (DEPTH, A_Q_HEADS), 0.5),
        "lam_q1": nrm(ks[8], (DEPTH, HEAD_DIM), 0.1),
        "lam_k1": nrm(ks[9], (DEPTH, HEAD_DIM), 0.1),
        "lam_q2": nrm(ks[10], (DEPTH, HEAD_DIM), 0.1),
        "lam_k2": nrm(ks[11], (DEPTH, HEAD_DIM), 0.1),
        "diff_subln": gain(ks[12], (DEPTH, B_V_DIM)),
        "w_out": nrm(ks[13], (DEPTH, MIX_WIDTH, D_MODEL), MIX_WIDTH ** -0.5),
        "ffn2_norm": gain(ks[14], (DEPTH, D_MODEL)),
        "ffn2_w_gate": nrm(ks[15], (DEPTH, D_MODEL, D_FF), D_MODEL ** -0.5),
        "ffn2_w_up": nrm(ks[16], (DEPTH, D_MODEL, D_FF), D_MODEL ** -0.5),
        "ffn2_w_down": nrm(ks[17], (DEPTH, D_FF, D_MODEL), D_FF ** -0.5),
        "final_norm": gain(ks[18], (D_MODEL,)),
    }


def reference(x, ffn1_norm, ffn1_w_gate, ffn1_w_up, ffn1_w_down, mix_norm, w_in, sink,
              lam_q1, lam_k1, lam_q2, lam_k2, diff_subln, w_out,
              ffn2_norm, ffn2_w_gate, ffn2_w_up, ffn2_w_down, final_norm):
    b, s, _ = x.shape
    slopes_a = alibi_slopes(A_Q_HEADS)
    slopes_b = alibi_slopes(B_HEADS)
    for l in range(DEPTH):
        x = x + 0.5 * swiglu(rms_norm(x, ffn1_norm[l]), ffn1_w_gate[l], ffn1_w_up[l], ffn1_w_down[l])

        h = rms_norm(x, mix_norm[l])
        proj = h @ w_in[l]
        qa, ka, va, qb, kb, vb = jnp.split(proj, SPLITS, axis=-1)
        qa = qa.reshape(b, s, A_Q_HEADS, HEAD_DIM)
        ka = ka.reshape(b, s, A_KV_HEADS, HEAD_DIM)
        va = va.reshape(b, s, A_KV_HEADS, HEAD_DIM)
        qb = qb.reshape(b, s, B_HEADS, 2, HEAD_DIM)
        kb = kb.reshape(b, s, B_HEADS, 2, HEAD_DIM)
        vb = vb.reshape(b, s, B_HEADS, B_V_DIM)

        out_a = windowed_gqa_sink(qa, ka, va, sink[l], slopes_a)

        lam_init = 0.8 - 0.6 * math.exp(-0.3 * l)
        lam = (jnp.exp(jnp.sum(lam_q1[l].astype(jnp.float32) * lam_k1[l].astype(jnp.float32)))
               - jnp.exp(jnp.sum(lam_q2[l].astype(jnp.float32) * lam_k2[l].astype(jnp.float32)))
               + lam_init)
        out_b = differential_attention(qb, kb, vb, lam, slopes_b)
        out_b = (rms_norm(out_b, diff_subln[l]) * (1.0 - lam_init)).reshape(b, s, B_V_COLS)

        x = x + jnp.concatenate([out_a, out_b], axis=-1) @ w_out[l]

        x = x + 0.5 * swiglu(rms_norm(x, ffn2_norm[l]), ffn2_w_gate[l], ffn2_w_up[l], ffn2_w_down[l])
    return rms_norm(x, final_norm)
```

```python
import math
import numpy as np
import ml_dtypes
import concourse.bass as bass
import concourse.mybir as mybir
from concourse.bass_utils import run_bass_kernel_spmd

F32 = mybir.dt.float32
BF16 = mybir.dt.bfloat16
AF = mybir.ActivationFunctionType
ALU = mybir.AluOpType

D = 1024
NT = 2048
TT = 512
NTT = NT // TT
DFF = 2816
NSL = DFF // 256
INC = 2304
DEPTH = 2
EPS = 1e-6
SKIP_T = 60.0
SLOPE_A = [2.0 ** (-(i + 1)) for i in range(8)]
SLOPE_B = [2.0 ** (-2 * (i + 1)) for i in range(4)]
L_ROW = 20480
OFF_KB, OFF_KA, OFF_VB, OFF_VA = 0, 8192, 10240, 18432
NSLOT = 6
TQ = 256
NTQ = NT // TQ
NBL = 31
NBIAS = 4 * NBL + 4 * 16
SAME_ENGINE_SYNC = True
DSEM_TOTAL = True


class Ins:
    __slots__ = ("eng", "fn", "kind", "dsem", "deps", "signals", "sig", "dcount", "inc")

    def __init__(self, eng, fn, kind, dsem, inc):
        self.eng = eng
        self.fn = fn
        self.kind = kind
        self.dsem = dsem
        self.deps = []
        self.signals = False
        self.sig = 0
        self.dcount = 0
        self.inc = inc


class Prog:
    ENGS = ("pe", "act", "dve", "pool", "sp")

    def __init__(self):
        self.ins = []
        self.last_w = {}
        self.readers = {}
        self.dsem_count = {}

    def add(self, eng, fn, reads=(), writes=(), kind="c", dsem=None, inc=16):
        I = Ins(eng, fn, kind, dsem, inc)
        deps = {}

        def dep(J, hazard):
            if J is None or J is I:
                return
            if J.kind == "c" and J.eng == eng and kind == "c":
                if eng == "pe":
                    return
                if not SAME_ENGINE_SYNC:
                    return
            deps[id(J)] = J

        for k in reads:
            dep(self.last_w.get(k), "raw")
        for k in writes:
            dep(self.last_w.get(k), "waw")
            for J in self.readers.get(k, {}).values():
                dep(J, "war")
        I.deps = [(J, ((self.dsem_count[J.dsem] if DSEM_TOTAL else J.dcount) if J.kind == "d" else 0)) for J in deps.values()]
        for J, _ in I.deps:
            J.signals = True
        if kind == "d":
            self.dsem_count[dsem] = self.dsem_count.get(dsem, 0) + inc
            I.dcount = self.dsem_count[dsem]
        for k in reads:
            r = self.readers.setdefault(k, {})
            r[(I.kind, I.eng if I.kind == "c" else I.dsem)] = I
        for k in writes:
            self.last_w[k] = I
            self.readers[k] = {}
        self.ins.append(I)
        return I

    def emit(self, nc, sems, dsems, final_waits):
        cnt = {e: 0 for e in self.ENGS}
        for I in self.ins:
            if I.kind == "c" and I.signals:
                cnt[I.eng] += 1
                I.sig = cnt[I.eng]
        streams = {e: [I for I in self.ins if I.eng == e] for e in self.ENGS}
        nwaits = [0]
        waitlog = self.waitlog = []

        def run(engname, eng):
            known = {}
            for I in streams[engname]:
                need = {}
                for J, dval in I.deps:
                    if J.kind == "c":
                        key = ("e", J.eng)
                        val = J.sig
                    else:
                        key = ("d", J.dsem)
                        val = dval
                    if val > need.get(key, 0):
                        need[key] = val
                for key, val in need.items():
                    if known.get(key, 0) >= val:
                        continue
                    known[key] = val
                    s = sems[key[1]] if key[0] == "e" else dsems[key[1]]
                    eng.wait_ge(s, val)
                    nwaits[0] += 1
                    waitlog.append((engname, key, val))
                bi = I.fn(eng)
                if I.kind == "d":
                    bi.then_inc(dsems[I.dsem], I.inc)
                elif I.signals:
                    bi.then_inc(sems[I.eng], 1)
            if engname in final_waits:
                for dname in final_waits[engname]:
                    eng.wait_ge(dsems[dname], self.dsem_count[dname])

        with nc.Block() as block:
            @block.tensor
            def _(e):
                run("pe", e)

            @block.scalar
            def _(e):
                run("act", e)

            @block.vector
            def _(e):
                run("dve", e)

            @block.gpsimd
            def _(e):
                run("pool", e)

            @block.sync
            def _(e):
                run("sp", e)
        return nwaits[0]


def make_tables():
    ki = np.arange(128, dtype=np.float64)[:, None]
    qi = np.arange(TQ, dtype=np.float64)[None, :]
    tabB = np.zeros((4, 128, 4, 2, TQ), np.float64)
    biasL = np.zeros((4, 128, NBL), np.float64)
    biasR = np.zeros((4, 128, 16), np.float64)
    for h in range(4):
        s = SLOPE_B[h]
        for o in range(2):
            tabB[h, :, o, :, :] = np.exp(-s * np.abs(qi - 128 * o - ki))[:, None, :]
        tabB[h, :, 2, :, :] = (np.exp(-s * qi) + 0 * ki)[:, None, :]
        tabB[h, :, 3, :, :] = (np.exp(-s * (TQ - 1 - qi)) + 0 * ki)[:, None, :]
        for n in range(NBL):
            biasL[h, :, n] = -s * (128 * n - ki[:, 0])
        for o in range(16):
            biasR[h, :, o] = -s * (128 * o - (TQ - 1) + ki[:, 0])
    tabB = tabB.reshape(4, 128, 4 * 2 * TQ)
    bias = np.concatenate([biasL.transpose(1, 0, 2).reshape(128, 4 * NBL),
                           biasR.transpose(1, 0, 2).reshape(128, 4 * 16)], axis=1)
    q1 = np.arange(128, dtype=np.float64)[None, :]
    tabA = np.zeros((8, 128, 4, 128), np.float64)
    for h in range(8):
        s = SLOPE_A[h]
        for oi, o in enumerate((-1, 0, 1)):
            d = np.abs(q1 - (128 * o + ki))
            tabA[h, :, oi, :] = np.exp(-s * d) * (d <= 128)
        d = 255 - ki - q1
        tabA[h, :, 3, :] = np.exp(-s * d) * (d <= 128)
    tabA = tabA.transpose(1, 0, 2, 3).reshape(128, 8 * 512)
    return (tabB.astype(ml_dtypes.bfloat16), bias.astype(np.float32), tabA.astype(ml_dtypes.bfloat16))


def build_nc(n_layers=DEPTH, groups=None, dbg=None, stop=None):
    groups = groups or [[0, 1], [2, 3], [4, 5], [6, 7]]
    nc = bass.Bass("TRN2", target_bir_lowering=False)
    P = Prog()
    ctxs = []

    def dram(name, shape, dt, kind):
        return nc.dram_tensor(name, list(shape), dt, kind=kind).ap()

    xT_d = dram("xT", [D, NT], F32, "ExternalInput")
    w_d = {}
    for nm, shp in (("ffn1_w_gate", [DEPTH, D, DFF]), ("ffn1_w_up", [DEPTH, D, DFF]), ("ffn1_w_down", [DEPTH, DFF, D]),
                    ("w_in", [DEPTH, D, INC]), ("w_out", [DEPTH, D, D]),
                    ("ffn2_w_gate", [DEPTH, D, DFF]), ("ffn2_w_up", [DEPTH, D, DFF]), ("ffn2_w_down", [DEPTH, DFF, D])):
        w_d[nm] = dram(nm, shp, F32, "ExternalInput")
    gains_d = dram("gains", [128, 7 * 8], F32, "ExternalInput")
    sinkc_d = dram("sinkc", [128, DEPTH * 4], F32, "ExternalInput")
    lamv_d = dram("lamv", [128, DEPTH * 4 * 64], F32, "ExternalInput")
    subln_d = dram("subln", [128, DEPTH], F32, "ExternalInput")
    mask_d = dram("mask", [128, 2], F32, "ExternalInput")
    tabB_d = dram("tabB", [4, 128, 2048], BF16, "ExternalInput")
    bias_d = dram("biasT", [128, NBIAS], F32, "ExternalInput")
    tabA_d = dram("tabA", [128, 8 * 512], BF16, "ExternalInput")
    out_d = dram("outT", [D, NT], F32, "ExternalOutput")
    qa_d = dram("qa_s", [4, 128, NT], BF16, "Internal")
    qb_d = dram("qb_s", [4, 128, NT], BF16, "Internal")
    own_d = dram("own_s", [128, L_ROW], BF16, "Internal")
    send_d = dram("send_s", [2 * 128, L_ROW], BF16, "Internal")
    recv_d = dram("recv_s", [128, L_ROW], BF16, "Internal")
    dbg_out = {}
    if dbg:
        for nm, shp, dt in dbg:
            dbg_out[nm] = dram(nm, shp, dt, "ExternalOutput")

    def sb(name, shape, dt):
        c = nc.sbuf_tensor(name, list(shape), dt)
        t = c.__enter__()
        ctxs.append(c)
        return t

    def psum(name):
        c = nc.psum_tensor(name, [128, 512], F32)
        t = c.__enter__()
        ctxs.append(c)
        return t

    xT = sb("xT_sb", [128, 8, NT], F32)
    hT = sb("hT_sb", [128, 8, NT], BF16)
    wsl = [sb(f"wsl{i}", [128, 2048], BF16) for i in range(NSLOT)]
    sq = sb("sq_sb", [128, 8, TT], BF16)
    rstd = sb("rstd_sb", [128, TT], F32)
    sg = [sb(f"sg{i}", [128, TT], F32) for i in range(2)]
    actT = [sb(f"actT{i}", [128, 2, TT], BF16) for i in range(2)]
    stg = [sb(f"stg{i}", [128, TT], BF16) for i in range(4)]
    stgm = [sb(f"stgm{i}", [128, 2, TT], BF16) for i in range(2)]
    gains = sb("gains_sb", [128, 7 * 8], F32)
    sinkc = sb("sinkc_sb", [128, DEPTH * 4], F32)
    esink = sb("esink_sb", [128, DEPTH * 4], F32)
    lamv = sb("lamv_sb", [128, DEPTH * 4 * 64], F32)
    lamt = sb("lamt_sb", [128, 64], F32)
    lams = sb("lams_sb", [128, 8], F32)
    neglam = sb("neglam_sb", [128, DEPTH], F32)
    subln = sb("subln_sb", [128, DEPTH], F32)
    maskc = sb("mask_sb", [128, 2], F32)
    biasT = sb("bias_sb", [128, NBIAS], F32)
    tabA2 = [sb(f"tabA_sb{i}", [128, 2, 4, 128], BF16) for i in range(2)]
    tabB = [sb(f"tabB{i}", [128, 2048], BF16) for i in range(1)]
    ones1024 = sb("ones1024", [128, 128], BF16)
    ones128 = sb("ones128", [128, 128], BF16)
    ones1 = sb("ones1", [128, 128], BF16)
    epsc = sb("epsc", [128, 1], F32)
    QB = [sb(f"QB{i}", [128, NT], BF16) for i in range(2)]
    KB = [sb(f"KB{i}", [128, 2 * NT], BF16) for i in range(1)]
    VB = [sb(f"VB{i}", [128, 32, 128], BF16) for i in range(1)]
    QA = [sb(f"QA{i}", [128, NT], BF16) for i in range(2)]
    KA = [sb(f"KA{i}", [128, NT + 128], BF16) for i in range(1)]
    VA = [sb(f"VA{i}", [128, 17, 64], BF16) for i in range(1)]
    fin3 = sb("fin3", [128, TT], F32)
    ytmp = [sb(f"ytmp{i}", [128, TT], F32) for i in range(2)]
    Eb = [actT[0][:, 0, :], actT[0][:, 1, :], actT[1][:, 0, :], actT[1][:, 1, :]]
    Pb = [stgm[0][:, 0, :], stgm[0][:, 1, :], stgm[1][:, 0, :], stgm[1][:, 1, :]]
    fin = [sg[0][:, :], sg[1][:, :], rstd[:, :], fin3[:, :]]
    EK = lambda i: ("act", i // 2, i % 2)
    PK = lambda i: ("stgm", i // 2, i % 2)
    FK = lambda i: (("sg", 0), ("sg", 1), ("rstd",), ("fin3",))[i]
    ps = [psum(f"ps{i}") for i in range(8)]

    def K(*a):
        return tuple(a)

    def PSK(b):
        return [("ps", b, 0), ("ps", b, 1)]

    def dma(eng, out, in_, reads, writes, dsem):
        P.add(eng, lambda e, out=out, in_=in_: e.dma_start(out=out, in_=in_), reads, writes, kind="d", dsem=dsem)

    def mm(out, lhsT, rhs, start, stop, reads, writes):
        P.add("pe", lambda e: e.matmul(out, lhsT, rhs, start=start, stop=stop), reads, writes)

    wlist = []

    def wsrc_cols(name, l, c0):
        return w_d[name][l, :, c0:c0 + 256].rearrange("(k p) n -> p k n", p=128)

    def wsrc_rows(name, l, r0):
        return w_d[name][l, r0:r0 + 256, :].rearrange("(k p) n -> p k n", p=128)

    WIN_ORDER = [5, 6, 7, 8, 2, 0, 1, 3, 4]
    for l in range(n_layers):
        for s in range(NSL):
            wlist.append((wsrc_cols("ffn1_w_gate", l, s * 256), 8))
            wlist.append((wsrc_cols("ffn1_w_up", l, s * 256), 8))
            wlist.append((wsrc_rows("ffn1_w_down", l, s * 256), 2))
        for i in WIN_ORDER:
            wlist.append((wsrc_cols("w_in", l, i * 256), 8))
        for i in range(4):
            wlist.append((wsrc_cols("w_out", l, i * 256), 8))
        for s in range(NSL):
            wlist.append((wsrc_cols("ffn2_w_gate", l, s * 256), 8))
            wlist.append((wsrc_cols("ffn2_w_up", l, s * 256), 8))
            wlist.append((wsrc_rows("ffn2_w_down", l, s * 256), 2))
    wstate = {"next": 0}

    def wview(idx):
        slot = idx % NSLOT
        a = wlist[idx][1]
        return wsl[slot][:, :].rearrange("p (a b) -> p a b", a=a), K("w", slot)

    def wensure(upto):
        upto = min(upto, len(wlist) - 1)
        while wstate["next"] <= upto:
            idx = wstate["next"]
            v, key = wview(idx)
            dma("pool", v, wlist[idx][0], [], [key], f"w{idx % NSLOT}")
            wstate["next"] += 1

    LA = 2
    wcur = {"i": 0}

    def wtake(n):
        i0 = wcur["i"]
        wcur["i"] += n
        wensure(i0 + n - 1 + LA)
        return [wview(i0 + j) for j in range(n)]

    for t in range(NTT):
        for c in range(8):
            dma("sp", xT[:, c, t * TT:(t + 1) * TT], xT_d[c * 128:(c + 1) * 128, t * TT:(t + 1) * TT], [], [K("x", c, t)], f"xin{t}")
    for t_, d_, nm in ((gains, gains_d, "gains"), (sinkc, sinkc_d, "sinkc"), (lamv, lamv_d, "lamv"), (subln, subln_d, "subln"),
                       (maskc, mask_d, "mask"), (biasT, bias_d, "bias")):
        dma("sp", t_[:, :], d_, [], [K(nm)], "small")
    P.add("pool", lambda e: e.memset(ones1024[:, :], 1.0 / 1024), [], [K("ones1024")])
    P.add("pool", lambda e: e.memset(ones128[:, :], 1.0 / 128), [], [K("ones128")])
    P.add("pool", lambda e: e.memset(ones1[:, :], 1.0), [], [K("ones1")])
    P.add("pool", lambda e: e.memset(epsc[:, :], EPS), [], [K("epsc")])
    P.add("pool", lambda e: e.memset(QB[0][64:128, :], 0.0), [], [K("QBz", 0)])
    P.add("pool", lambda e: e.memset(QB[1][0:64, :], 0.0), [], [K("QBz", 1)])
    wensure(LA)
    P.add("act", lambda e: e.activation(out=esink[:, :], in_=sinkc[:, :], func=AF.Exp), [K("sinkc")], [K("esink")])
    for l in range(n_layers):
        lam_init = 0.8 - 0.6 * math.exp(-0.3 * l)
        for j in range(2):
            a = lamv[:, (l * 4 + 2 * j) * 64:(l * 4 + 2 * j + 1) * 64]
            b = lamv[:, (l * 4 + 2 * j + 1) * 64:(l * 4 + 2 * j + 2) * 64]
            P.add("dve", lambda e, a=a, b=b: e.tensor_tensor(out=lamt[:, :], in0=a, in1=b, op=ALU.mult), [K("lamv")], [K("lamt")])
            P.add("dve", lambda e, l=l, j=j: e.reduce_sum(out=lams[:, l * 4 + j:l * 4 + j + 1], in_=lamt[:, :], axis=mybir.AxisListType.X),
                  [K("lamt")], [K("lams", l, j)])
            P.add("act", lambda e, l=l, j=j: e.activation(out=lams[:, l * 4 + 2 + j:l * 4 + 3 + j], in_=lams[:, l * 4 + j:l * 4 + j + 1], func=AF.Exp),
                  [K("lams", l, j)], [K("lame", l, j)])
        P.add("dve", lambda e, l=l, li=lam_init: e.scalar_tensor_tensor(out=neglam[:, l:l + 1], in0=lams[:, l * 4 + 3:l * 4 + 4], scalar=-li,
                                                                        in1=lams[:, l * 4 + 2:l * 4 + 3], op0=ALU.add, op1=ALU.subtract),
              [K("lame", l, 0), K("lame", l, 1)], [K("neglam", l)])
        P.add("dve", lambda e, l=l, li=lam_init: e.tensor_scalar_mul(out=subln[:, l:l + 1], in0=subln[:, l:l + 1], scalar1=1.0 - li),
              [K("subln")], [K("subln")])

    PS_ST = 6

    def emit_norm(gcol, tiles=range(NTT), final=False):
        for t in tiles:
            ts = slice(t * TT, (t + 1) * TT)
            for c in range(8):
                P.add("act", lambda e, c=c, ts=ts: e.activation(out=sq[:, c, :], in_=xT[:, c, ts], func=AF.Square),
                      [K("x", c, t)], [K("sq", c)])
            for c in range(8):
                mm(ps[PS_ST][:, :], ones1024[:, :], sq[:, c, :], c == 0, c == 7, [K("sq", c), K("ones1024")], [*PSK(PS_ST)])
            P.add("act", lambda e: e.activation(out=rstd[:, :], in_=ps[PS_ST][:, :], func=AF.Sqrt, bias=epsc[:, 0:1], scale=1.0),
                  [*PSK(PS_ST), K("epsc")], [K("rstd")])
            P.add("dve", lambda e: e.reciprocal(out=rstd[:, :], in_=rstd[:, :]), [K("rstd")], [K("rstd")])
            for c in range(8):
                g = gains[:, gcol * 8 + c:gcol * 8 + c + 1]
                if not final:
                    P.add("dve", lambda e, c=c, ts=ts, g=g: e.scalar_tensor_tensor(out=hT[:, c, ts], in0=xT[:, c, ts], scalar=g, in1=rstd[:, :],
                                                                                   op0=ALU.mult, op1=ALU.mult),
                          [K("x", c, t), K("rstd"), K("gains")], [K("h", c, t)])
                else:
                    P.add("dve", lambda e, c=c, ts=ts, g=g: e.scalar_tensor_tensor(out=xT[:, c, ts], in0=xT[:, c, ts], scalar=g, in1=rstd[:, :],
                                                                                   op0=ALU.mult, op1=ALU.mult),
                          [K("x", c, t), K("rstd"), K("gains")], [K("x", c, t)])

    def emit_ffn():
        its = [(s, t) for s in range(NSL) for t in range(NTT)]
        wv = {}

        def GU(n):
            s, t = its[n]
            if t == 0:
                wv[s] = wtake(3)
            (wg, kg), (wu, ku), _ = wv[s]
            ts = slice(t * TT, (t + 1) * TT)
            for j in range(2):
                pg, pu = (n * 2 + j) % 2, 2 + (n * 2 + j) % 2
                for k in range(8):
                    mm(ps[pg][:, :], wg[:, k, j * 128:(j + 1) * 128], hT[:, k, ts], k == 0, k == 7, [kg, K("h", k, t)], [*PSK(pg)])
                for k in range(8):
                    mm(ps[pu][:, :], wu[:, k, j * 128:(j + 1) * 128], hT[:, k, ts], k == 0, k == 7, [ku, K("h", k, t)], [*PSK(pu)])
                sgb = (n * 2 + j) % 2
                P.add("act", lambda e, pg=pg, sgb=sgb: e.activation(out=sg[sgb][:, :], in_=ps[pg][:, :], func=AF.Silu), [*PSK(pg)], [K("sg", sgb)])
                P.add("dve", lambda e, pu=pu, sgb=sgb, n=n, j=j: e.tensor_tensor(out=actT[n % 2][:, j, :], in0=ps[pu][:, :], in1=sg[sgb][:, :], op=ALU.mult),
                      [*PSK(pu), K("sg", sgb)], [K("act", n % 2, j)])

        def DN(n):
            s, t = its[n]
            (wd, kd) = wv[s][2]
            ts = slice(t * TT, (t + 1) * TT)
            for m in range(8):
                py = (4, 5, 7, 6)[m % 4]
                for j in range(2):
                    mm(ps[py][:, :], wd[:, j, m * 128:(m + 1) * 128], actT[n % 2][:, j, :], j == 0, j == 1, [kd, K("act", n % 2, j)], [*PSK(py)])
                if m % 2 == 0:
                    P.add("dve", lambda e, m=m, ts=ts, py=py: e.scalar_tensor_tensor(out=xT[:, m, ts], in0=ps[py][:, :], scalar=0.5, in1=xT[:, m, ts],
                                                                                    op0=ALU.mult, op1=ALU.add),
                          [*PSK(py), K("x", m, t)], [K("x", m, t)])
                else:
                    yb = (m // 2) % 2
                    P.add("act", lambda e, py=py, yb=yb: e.mul(out=ytmp[yb][:, :], in_=ps[py][:, :], mul=0.5), [*PSK(py)], [K("ytmp", yb)])
                    P.add("pool", lambda e, m=m, ts=ts, yb=yb: e.tensor_tensor(out=xT[:, m, ts], in0=xT[:, m, ts], in1=ytmp[yb][:, :], op=ALU.add),
                          [K("ytmp", yb), K("x", m, t)], [K("x", m, t)])

        for n in range(len(its) + 1):
            if n < len(its):
                GU(n)
            if n >= 1:
                DN(n - 1)

    stg_i = {"i": 0, "m": 0}
    OWN_KB = lambda h: [K("own", "kb", h, t) for t in range(NTT)]
    OWN_VB = [K("own", "vb", tb) for tb in range(16)]
    OWN_KA = [K("own", "ka", t) for t in range(NTT)]
    OWN_VA = [K("own", "va", tb) for tb in range(16)]
    SEND_KEYS = [K("send", "kb", h, t, r) for h in range(4) for t in range(NTT) for r in range(2)] + \
                [K("send", "vb", tb, r) for tb in range(16) for r in range(2)] + \
                [K("send", "ka", t, r) for t in range(NTT) for r in range(2)] + \
                [K("send", "va", tb, r) for tb in range(16) for r in range(2)]

    def emit_proj(l):
        wv = {}
        order = list(WIN_ORDER)

        def need(i):
            if i not in wv:
                assert order.pop(0) == i
                wv[i] = wtake(1)[0]
            return wv[i]

        def fm_chunk(i, half, dests):
            w, kw = need(i)
            for t in range(NTT):
                ts = slice(t * TT, (t + 1) * TT)
                pb = (stg_i["i"]) % 4
                for k in range(8):
                    mm(ps[pb][:, :], w[:, k, half * 128:(half + 1) * 128], hT[:, k, ts], k == 0, k == 7, [kw, K("h", k, t)], [*PSK(pb)])
                si = stg_i["i"] % 4
                stg_i["i"] += 1
                P.add("act", lambda e, si=si, pb=pb: e.copy(out=stg[si][:, :], in_=ps[pb][:, :]), [*PSK(pb)], [K("stg", si)])
                for (dfn, masked, dkeyf) in dests:
                    dkey = dkeyf(t)
                    if not masked:
                        dma("sp", dfn(t), stg[si][:, :], [K("stg", si)], [dkey], f"st{si}")
                    else:
                        mi = stg_i["m"] % 2
                        stg_i["m"] += 1
                        for r in range(2):
                            P.add("dve", lambda e, si=si, mi=mi, r=r: e.tensor_scalar_mul(out=stgm[mi][:, r, :], in0=stg[si][:, :], scalar1=maskc[:, r:r + 1]),
                                  [K("stg", si), K("mask")], [K("stgm", mi, r)])
                            dma("sp", dfn(t, r), stgm[mi][:, r, :], [K("stgm", mi, r)], [dkey + (r,)], f"sm{mi}")

        def tm_block(i_list, col0, ncols, dests):
            for i in i_list:
                need(i)
            for tb in range(16):
                t = tb // 4
                tsl = slice(tb * 128, (tb + 1) * 128)
                pb = (stg_i["i"]) % 4
                for ii, i in enumerate(i_list):
                    for k in range(8):
                        w, kw = wv[i]
                        n = min(256, ncols - ii * 256)
                        mm(ps[pb][:, ii * 256:ii * 256 + n], hT[:, k, tsl], w[:, k, col0:col0 + n], k == 0, k == 7, [kw, K("h", k, t)], [*PSK(pb)])
                si = stg_i["i"] % 4
                stg_i["i"] += 1
                P.add("act", lambda e, si=si, pb=pb, ncols=ncols: e.copy(out=stg[si][:, 0:ncols], in_=ps[pb][:, 0:ncols]), [*PSK(pb)], [K("stg", si)])
                for (dfn, masked, dkeyf) in dests:
                    dkey = dkeyf(tb)
                    if not masked:
                        dma("sp", dfn(tb), stg[si][:, 0:ncols], [K("stg", si)], [dkey], f"st{si}")
                    else:
                        mi = stg_i["m"] % 2
                        stg_i["m"] += 1
                        for r in range(2):
                            P.add("dve", lambda e, si=si, mi=mi, r=r, ncols=ncols: e.tensor_scalar_mul(out=stgm[mi][:, r, 0:ncols], in0=stg[si][:, 0:ncols],
                                                                                                  scalar1=maskc[:, r:r + 1]),
                                  [K("stg", si), K("mask")], [K("stgm", mi, r)])
                            dma("sp", dfn(tb, r), stgm[mi][:, r, 0:ncols], [K("stgm", mi, r)], [dkey + (r,)], f"sm{mi}")

        def ownfm(off):
            return lambda t: own_d[:, off + t * TT: off + (t + 1) * TT]

        def sendfm(off):
            return lambda t, r: send_d[r * 128:(r + 1) * 128, off + t * TT: off + (t + 1) * TT]

        for h in range(4):
            i, half = 5 + h // 2, h % 2
            fm_chunk(i, half, [(ownfm(OFF_KB + h * NT), False, lambda t, h=h: K("own", "kb", h, t)),
                               (sendfm(OFF_KB + h * NT), True, lambda t, h=h: K("send", "kb", h, t))])
        tm_block([7, 8], 0, 512,
                 [(lambda tb: own_d[:, OFF_VB + tb * 512: OFF_VB + (tb + 1) * 512], False, lambda tb: K("own", "vb", tb)),
                  (lambda tb, r: send_d[r * 128:(r + 1) * 128, OFF_VB + tb * 512: OFF_VB + (tb + 1) * 512], True, lambda tb: K("send", "vb", tb))])
        fm_chunk(2, 0, [(ownfm(OFF_KA), False, lambda t: K("own", "ka", t)), (sendfm(OFF_KA), True, lambda t: K("send", "ka", t))])
        tm_block([2], 128, 128,
                 [(lambda tb: own_d[:, OFF_VA + tb * 128: OFF_VA + (tb + 1) * 128], False, lambda tb: K("own", "va", tb)),
                  (lambda tb, r: send_d[r * 128:(r + 1) * 128, OFF_VA + tb * 128: OFF_VA + (tb + 1) * 128], True, lambda tb: K("send", "va", tb))])
        P.add("pool", lambda e: e.collective_compute("ReduceScatter", ALU.add, replica_groups=groups, ins=[send_d], outs=[recv_d]),
              SEND_KEYS, [K("recv")], kind="d", dsem="cc", inc=1)
        for c in range(4):
            fm_chunk(c // 2, c % 2, [(lambda t, c=c: qa_d[c, :, t * TT:(t + 1) * TT], False, lambda t, c=c: K("qa", c, t))])
        for h in range(4):
            fm_chunk(3 + h // 2, h % 2, [(lambda t, h=h: qb_d[h, :, t * TT:(t + 1) * TT], False, lambda t, h=h: K("qb", h, t))])

    def emit_attn_a(l):
        LAG = 3
        for c in range(4):
            kv = c // 2
            cb = c % 2
            QAc, tabAc = QA[cb], tabA2[cb]
            dma("sp", QAc[:, :], qa_d[c, :, :], [K("qa", c, t) for t in range(NTT)], [K("QA", cb)], f"QA{cb}")
            dma("sp", tabAc[:, :, :, :], tabA_d[:, c * 1024:(c + 1) * 1024].rearrange("p (h o q) -> p h o q", h=2, o=4), [], [K("tabA", cb)], f"QA{cb}")
            if c % 2 == 0:
                for hh in range(2):
                    dma("sp", KA[0][hh * 64:(hh + 1) * 64, 0:NT], own_d[kv * 64:(kv + 1) * 64, OFF_KA:OFF_KA + NT], OWN_KA, [K("KA", hh)], "KA")
                    dma("pool", KA[0][hh * 64:(hh + 1) * 64, NT:NT + 128], recv_d[kv * 64:(kv + 1) * 64, OFF_KA + NT - 128:OFF_KA + NT], [K("recv")], [K("KAp", hh)], "KAp")
                dma("sp", VA[0][:, 0:16, :], own_d[:, OFF_VA:OFF_VA + 2048].rearrange("p (b c) -> p b c", c=128)[:, :, kv * 64:(kv + 1) * 64],
                    OWN_VA, [K("VA")], "VA")
                dma("pool", VA[0][:, 16, :], recv_d[:, OFF_VA + 15 * 128 + kv * 64: OFF_VA + 15 * 128 + (kv + 1) * 64], [K("recv")], [K("VAp")], "VAp")
            kreads = [K("KA", 0), K("KA", 1), K("KAp", 0), K("KAp", 1)]
            vreads = [K("VA"), K("VAp")]
            its = [(qb, hh) for qb in range(16) for hh in range(2)]

            def blocks_of(qb):
                blocks = []
                if qb >= 1:
                    blocks.append(((qb - 1) * 128, qb - 1, 0))
                blocks.append((qb * 128, qb, 1))
                if qb <= 14:
                    blocks.append(((qb + 1) * 128, qb + 1, 2))
                else:
                    blocks.append((NT, 16, 3))
                return blocks

            def S(n):
                qb, hh = its[n]
                qs = slice(qb * 128, (qb + 1) * 128)
                blocks = blocks_of(qb)
                nb = len(blocks)
                hp = slice(hh * 64, (hh + 1) * 64)
                pss = n % 4
                eb = n % 4
                for bi, (kc, vb, ti) in enumerate(blocks):
                    kr = [K("KAp", 0), K("KAp", 1)] if ti == 3 else [K("KA", 0), K("KA", 1)]
                    mm(ps[pss][:, bi * 128:(bi + 1) * 128], KA[0][hp, kc:kc + 128], QAc[hp, qs], True, True, kr + [K("QA", cb)], [*PSK(pss)])
                P.add("act", lambda e, eb=eb, pss=pss, nb=nb: e.activation(out=Eb[eb][:, 0:nb * 128], in_=ps[pss][:, 0:nb * 128], func=AF.Exp, scale=0.125),
                      [*PSK(pss)], [EK(eb)])
                t0 = blocks[0][2]
                if [b[2] for b in blocks] == list(range(t0, t0 + nb)):
                    P.add("dve", lambda e, eb=eb, hh=hh, t0=t0, nb=nb, tab=tabAc: e.tensor_tensor(out=Pb[eb][:, 0:nb * 128], in0=Eb[eb][:, 0:nb * 128],
                                                                                       in1=tab[:, hh, t0:t0 + nb, :].rearrange("p a b -> p (a b)"), op=ALU.mult),
                          [EK(eb), K("tabA", cb)], [PK(eb)])
                else:
                    for bi, (kc, vb, ti) in enumerate(blocks):
                        P.add("dve", lambda e, eb=eb, hh=hh, ti=ti, bi=bi, tab=tabAc: e.tensor_tensor(out=Pb[eb][:, bi * 128:(bi + 1) * 128], in0=Eb[eb][:, bi * 128:(bi + 1) * 128],
                                                                                         in1=tab[:, hh, ti, :], op=ALU.mult),
                              [EK(eb), K("tabA", cb)], [PK(eb)])

            def PV(n):
                qb, hh = its[n]
                t = qb // 4
                qs = slice(qb * 128, (qb + 1) * 128)
                blocks = blocks_of(qb)
                nb = len(blocks)
                hp = slice(hh * 64, (hh + 1) * 64)
                eb = n % 4
                po, pl = 4 + (qb % 2), 6 + (qb % 2)
                for bi, (kc, vb, ti) in enumerate(blocks):
                    vr = [K("VAp")] if ti == 3 else [K("VA")]
                    mm(ps[po][hp, 0:128], VA[0][:, vb, :], Pb[eb][:, bi * 128:(bi + 1) * 128], bi == 0, bi == nb - 1, vr + [PK(eb)], [K("ps", po, hh)])
                for bi, (kc, vb, ti) in enumerate(blocks):
                    mm(ps[pl][hp, 0:128], ones1[:, 0:64], Pb[eb][:, bi * 128:(bi + 1) * 128], bi == 0, bi == nb - 1, [K("ones1"), PK(eb)], [K("ps", pl, hh)])
                if hh == 1:
                    fb = qb % 2
                    P.add("dve", lambda e, fb=fb, pl=pl, c=c: e.tensor_scalar_add(out=fin[fb][:, 0:128], in0=ps[pl][:, 0:128], scalar1=esink[:, l * 4 + c:l * 4 + c + 1]),
                          [K("ps", pl, 0), K("ps", pl, 1), K("esink")], [FK(fb)])
                    P.add("dve", lambda e, fb=fb: e.reciprocal(out=fin[fb][:, 0:128], in_=fin[fb][:, 0:128]), [FK(fb)], [FK(fb)])
                    P.add("dve", lambda e, fb=fb, po=po, c=c, qs=qs: e.tensor_tensor(out=hT[:, c, qs], in0=ps[po][:, 0:128], in1=fin[fb][:, 0:128], op=ALU.mult),
                          [K("ps", po, 0), K("ps", po, 1), FK(fb)], [K("h", c, t)])

            N = len(its)
            for n in range(N + LAG):
                if n < N:
                    S(n)
                if n >= LAG:
                    PV(n - LAG)

    def emit_attn_b(l):
        LAG = 3
        NSB = 4
        for h in range(4):
            slope = SLOPE_B[h]
            hb = 0
            dma("sp", QB[0][0:64, :], qb_d[h, 0:64, :], [K("qb", h, t) for t in range(NTT)], [K("QB", 0)], "QB0")
            dma("sp", QB[1][64:128, :], qb_d[h, 64:128, :], [K("qb", h, t) for t in range(NTT)], [K("QB", 1)], "QB0")
            dma("sp", KB[hb][:, 0:NT], own_d[:, OFF_KB + h * NT: OFF_KB + (h + 1) * NT], OWN_KB(h), [K("KBo", hb)], f"KB{hb}")
            dma("pool", KB[hb][:, NT:2 * NT], recv_d[:, OFF_KB + h * NT: OFF_KB + (h + 1) * NT], [K("recv")], [K("KBp", hb)], "KBp")
            dma("sp", tabB[0][:, :], tabB_d[h, :, :], [], [K("tabB")], "tabB")
            dma("sp", VB[hb][:, 0:16, :], own_d[:, OFF_VB:OFF_VB + 8192].rearrange("p (b c) -> p b c", c=512)[:, :, h * 128:(h + 1) * 128],
                OWN_VB, [K("VBo", hb)], f"VB{hb}")
            dma("pool", VB[hb][:, 16:32, :], recv_d[:, OFF_VB:OFF_VB + 8192].rearrange("p (b c) -> p b c", c=512)[:, :, h * 128:(h + 1) * 128],
                [K("recv")], [K("VBp", hb)], "VBp")
            kreads = [K("KBo", hb), K("KBp", hb), K("QB", 0), K("QB", 1), K("QBz", 0), K("QBz", 1)]
            vreads = [K("VBo", hb), K("VBp", hb)]
            its = []
            for tq in range(NTQ):
                blocks = []
                for j in range(16):
                    o = j - 2 * tq
                    if 0 <= o <= 1:
                        blocks.append((j * 128, j, None, o * 512))
                    elif o < 0:
                        n = -o
                        if slope * (128 * n - 127) > SKIP_T:
                            continue
                        blocks.append((j * 128, j, h * NBL + n, 1024))
                    else:
                        if slope * (128 * o - (TQ - 1)) > SKIP_T:
                            continue
                        blocks.append((j * 128, j, 4 * NBL + h * 16 + o, 1536))
                for j in range(16):
                    v = 30 - 2 * tq - j
                    if slope * (128 * v - 127) > SKIP_T:
                        continue
                    blocks.append((NT + j * 128, 16 + j, h * NBL + v, 1536))
                nb = len(blocks)
                for bi, (kc, vb, bcol, tcol) in enumerate(blocks):
                    its.append((tq, kc, vb, bcol, tcol, bi == 0, bi == nb - 1))

            def S(n):
                tq, kc, vb, bcol, tcol, first, last = its[n]
                qs = slice(tq * TQ, (tq + 1) * TQ)
                pss = n % NSB
                eb = n % 4
                for ty in range(2):
                    kr = [K("KBp", hb) if kc >= NT else K("KBo", hb), K("QB", 0), K("QB", 1), K("QBz", 0), K("QBz", 1)]
                    mm(ps[pss][:, ty * TQ:(ty + 1) * TQ], KB[hb][:, kc:kc + 128], QB[ty][:, qs], True, True, kr, [*PSK(pss)])
                if bcol is None:
                    P.add("act", lambda e, eb=eb, pss=pss: e.activation(out=Eb[eb][:, :], in_=ps[pss][:, :], func=AF.Exp, scale=0.125),
                          [*PSK(pss)], [EK(eb)])
                else:
                    P.add("act", lambda e, eb=eb, pss=pss, bcol=bcol: e.activation(out=Eb[eb][:, :], in_=ps[pss][:, :], func=AF.Exp, scale=0.125,
                                                                                   bias=biasT[:, bcol:bcol + 1]),
                          [*PSK(pss), K("bias")], [EK(eb)])
                P.add("dve", lambda e, eb=eb, tcol=tcol: e.tensor_tensor(out=Pb[eb][:, :], in0=Eb[eb][:, :], in1=tabB[0][:, tcol:tcol + 512], op=ALU.mult),
                      [EK(eb), K("tabB")], [PK(eb)])

            pending = []

            def stage_b(tq):
                po = 4 + 2 * (tq % 2)
                P.add("dve", lambda e, po=po: e.tensor_tensor(out=fin[0][:, :], in0=ps[po][:, :], in1=fin[0][:, :], op=ALU.mult), [*PSK(po), FK(0)], [FK(0)])
                P.add("dve", lambda e: e.scalar_tensor_tensor(out=fin[2][:, 0:TQ], in0=fin[0][:, TQ:2 * TQ], scalar=neglam[:, l:l + 1], in1=fin[0][:, 0:TQ],
                                                              op0=ALU.mult, op1=ALU.add),
                      [FK(0), K("neglam", l)], [FK(2)])
                P.add("pool", lambda e: e.tensor_tensor(out=sq[:, 0, 0:TQ], in0=fin[2][:, 0:TQ], in1=fin[2][:, 0:TQ], op=ALU.mult), [FK(2)], [K("sq", 0)])

            def stage_c(tq):
                t = tq // 2
                qs = slice(tq * TQ, (tq + 1) * TQ)
                pl = 5 + 2 * (tq % 2)
                mm(ps[pl][:, 0:TQ], ones128[:, :], sq[:, 0, 0:TQ], True, True, [K("sq", 0), K("ones128")], [*PSK(pl)])
                P.add("act", lambda e, pl=pl: e.activation(out=fin[3][:, 0:TQ], in_=ps[pl][:, 0:TQ], func=AF.Ln, bias=epsc[:, 0:1], scale=1.0),
                      [*PSK(pl), K("epsc")], [FK(3)])
                P.add("act", lambda e: e.activation(out=fin[3][:, 0:TQ], in_=fin[3][:, 0:TQ], func=AF.Exp, scale=-0.5), [FK(3)], [FK(3)])
                P.add("dve", lambda e, h=h, qs=qs: e.scalar_tensor_tensor(out=hT[:, 4 + h, qs], in0=fin[2][:, 0:TQ], scalar=subln[:, l:l + 1], in1=fin[3][:, 0:TQ],
                                                                          op0=ALU.mult, op1=ALU.mult),
                      [FK(2), FK(3), K("subln")], [K("h", 4 + h, t)])

            def run_pending(upto):
                while pending and pending[0][0] <= upto:
                    due, fn, tq = pending.pop(0)
                    fn(tq)
                    if fn is stage_b:
                        pending.append((due + 3, stage_c, tq))

            def PV(n):
                tq, kc, vb, bcol, tcol, first, last = its[n]
                eb = n % 4
                po, pl = 4 + 2 * (tq % 2), 5 + 2 * (tq % 2)
                vr = [K("VBp", hb) if vb >= 16 else K("VBo", hb)]
                mm(ps[po][:, :], VB[hb][:, vb, :], Pb[eb][:, :], first, last, vr + [PK(eb)], [*PSK(po)])
                mm(ps[pl][:, :], ones1[:, :], Pb[eb][:, :], first, last, [K("ones1"), PK(eb)], [*PSK(pl)])
                if last:
                    run_pending(10 ** 9)
                    P.add("act", lambda e, pl=pl: e.activation(out=fin[0][:, :], in_=ps[pl][:, :], func=AF.Ln), [*PSK(pl)], [FK(0)])
                    P.add("act", lambda e: e.activation(out=fin[0][:, :], in_=fin[0][:, :], func=AF.Exp, scale=-1.0), [FK(0)], [FK(0)])
                    pending.append((n + 1, stage_b, tq))

            N = len(its)
            for n in range(N + LAG):
                if n < N:
                    S(n)
                if n >= LAG:
                    PV(n - LAG)
                    run_pending(n - LAG)
            run_pending(10 ** 9)
            run_pending(10 ** 9)

    def emit_wout():
        wv = wtake(4)
        for i in range(4):
            w, kw = wv[i]
            for t in range(NTT):
                ts = slice(t * TT, (t + 1) * TT)
                for mh in range(2):
                    m = 2 * i + mh
                    py = (t * 2 + mh) % 4
                    for k in range(8):
                        mm(ps[py][:, :], w[:, k, mh * 128:(mh + 1) * 128], hT[:, k, ts], k == 0, k == 7, [kw, K("h", k, t)], [*PSK(py)])
                    P.add("dve", lambda e, m=m, ts=ts, py=py: e.tensor_tensor(out=xT[:, m, ts], in0=ps[py][:, :], in1=xT[:, m, ts], op=ALU.add),
                          [*PSK(py), K("x", m, t)], [K("x", m, t)])

    def forward():
        for l in range(n_layers):
            emit_norm(0 + l)
            emit_ffn()
            if stop == ("ffn1", l):
                return
            emit_norm(2 + l)
            emit_proj(l)
            emit_attn_a(l)
            emit_attn_b(l)
            if stop == ("attn", l):
                return
            emit_wout()
            if stop == ("wout", l):
                return
            emit_norm(4 + l)
            emit_ffn()
        emit_norm(6, final=True)

    forward()
    for t in range(NTT):
        for c in range(8):
            dma("sp", out_d[c * 128:(c + 1) * 128, t * TT:(t + 1) * TT], xT[:, c, t * TT:(t + 1) * TT], [K("x", c, t)], [K("out", c, t)], "out")
    if dbg:
        for nm, shp, dt in dbg:
            if nm == "dbg_qa":
                dma("sp", dbg_out[nm], qa_d, [K("qa", c, t) for c in range(4) for t in range(NTT)], [K("dbgo1")], "out")
            if nm == "dbg_qb":
                dma("sp", dbg_out[nm], qb_d, [K("qb", c, t) for c in range(4) for t in range(NTT)], [K("dbgo2")], "out")
            if nm == "dbg_own":
                dma("sp", dbg_out[nm], own_d, [kk for h in range(4) for kk in OWN_KB(h)] + OWN_VB + OWN_KA + OWN_VA, [K("dbgo3")], "out")
            if nm == "dbg_recv":
                dma("sp", dbg_out[nm], recv_d, [K("recv")], [K("dbgo4")], "out")
            if nm == "dbg_h":
                dma("sp", dbg_out[nm].rearrange("(c p) n -> p c n", p=128), hT[:, :, :], [K("h", c, t) for c in range(8) for t in range(NTT)], [K("dbgo")], "out")

    dnames = sorted(P.dsem_count.keys())
    sem_ctx = {}
    for nme in list(Prog.ENGS[:4]) + ["d_" + d for d in dnames]:
        c = nc.semaphore(nme)
        sem_ctx[nme] = c.__enter__()
        ctxs.append(c)
    sems = {e: sem_ctx[e] for e in Prog.ENGS[:4]}
    dsems = {d: sem_ctx["d_" + d] for d in dnames}
    nw = P.emit(nc, sems, dsems, {"sp": ["out"]})
    for c in reversed(ctxs):
        c.__exit__(None, None, None)
    nc._n_ins = len(P.ins)
    nc._n_waits = nw
    nc._waitlog = P.waitlog
    return nc


_TABLES = None


def make_in_maps(inputs):
    global _TABLES
    if _TABLES is None:
        _TABLES = make_tables()
    tabB, bias, tabA = _TABLES
    f = lambda a: np.ascontiguousarray(np.asarray(a, dtype=np.float32))
    x = f(inputs["x"])
    gl = []
    for nm in ("ffn1_norm", "mix_norm", "ffn2_norm"):
        g = f(inputs[nm])
        for l in range(DEPTH):
            gl.append(g[l].reshape(8, 128).T)
    gl.append(f(inputs["final_norm"]).reshape(8, 128).T)
    gains = np.ascontiguousarray(np.concatenate(gl, axis=1))
    sink = f(inputs["sink"])
    sinkc = np.zeros((128, DEPTH * 4), np.float32)
    for l in range(DEPTH):
        for c in range(4):
            sinkc[0:64, l * 4 + c] = sink[l, 2 * c]
            sinkc[64:128, l * 4 + c] = sink[l, 2 * c + 1]
    lamv = np.concatenate([np.broadcast_to(f(inputs[nm])[l][None, :], (128, 64)) for l in range(DEPTH)
                           for nm in ("lam_q1", "lam_k1", "lam_q2", "lam_k2")], axis=1)
    lamv = np.ascontiguousarray(lamv)
    subln = np.ascontiguousarray(f(inputs["diff_subln"]).T)
    wts = {nm: f(inputs[nm]) for nm in ("ffn1_w_gate", "ffn1_w_up", "ffn1_w_down", "w_in", "w_out",
                                        "ffn2_w_gate", "ffn2_w_up", "ffn2_w_down")}
    maps = []
    for c in range(8):
        b, r = c // 2, c % 2
        xs = x[b, r * NT:(r + 1) * NT, :]
        if r == 1:
            xs = xs[::-1]
        m = dict(wts)
        m["xT"] = np.ascontiguousarray(xs.T)
        m["gains"] = gains
        m["sinkc"] = sinkc
        m["lamv"] = lamv
        m["subln"] = subln
        mk = np.zeros((128, 2), np.float32)
        mk[:, 1 - r] = 1.0
        m["mask"] = mk
        m["tabB"] = tabB
        m["biasT"] = bias
        m["tabA"] = tabA
        maps.append(m)
    return maps


def assemble(results):
    out = np.zeros((4, 2 * NT, D), np.float32)
    for c in range(8):
        b, r = c // 2, c % 2
        o = np.asarray(results[c]["outT"]).T
        if r == 1:
            o = o[::-1]
        out[b, r * NT:(r + 1) * NT, :] = o
    return out


_NC = None


def kernel(**inputs):
    global _NC
    if _NC is None:
        _NC = build_nc()
    maps = make_in_maps(inputs)
    res = run_bass_kernel_spmd(_NC, maps, core_ids=list(range(8)))
    return assemble(res.results)
```

```python
import math
import numpy as np
import ml_dtypes
import concourse.bass as bass
import concourse.mybir as mybir
from concourse.bass_utils import run_bass_kernel_spmd

F32 = mybir.dt.float32
BF16 = mybir.dt.bfloat16
AF = mybir.ActivationFunctionType
ALU = mybir.AluOpType

D = 1024
NT = 2048
TT = 512
NTT = NT // TT
DFF = 2816
NSL = DFF // 256
INC = 2304
DEPTH = 2
EPS = 1e-6
SKIP_T = 60.0
SLOPE_A = [2.0 ** (-(i + 1)) for i in range(8)]
SLOPE_B = [2.0 ** (-2 * (i + 1)) for i in range(4)]
L_ROW = 20480
OFF_KB, OFF_KA, OFF_VB, OFF_VA = 0, 8192, 10240, 18432
NSLOT = 6
TQ = 256
NTQ = NT // TQ
NBL = 31
NBIAS = 4 * NBL + 4 * 16
SAME_ENGINE_SYNC = True
DSEM_TOTAL = True


class Ins:
    __slots__ = ("eng", "fn", "kind", "dsem", "deps", "signals", "sig", "dcount", "inc")

    def __init__(self, eng, fn, kind, dsem, inc):
        self.eng = eng
        self.fn = fn
        self.kind = kind
        self.dsem = dsem
        self.deps = []
        self.signals = False
        self.sig = 0
        self.dcount = 0
        self.inc = inc


class Prog:
    ENGS = ("pe", "act", "dve", "pool", "sp")

    def __init__(self):
        self.ins = []
        self.last_w = {}
        self.readers = {}
        self.dsem_count = {}

    def add(self, eng, fn, reads=(), writes=(), kind="c", dsem=None, inc=16):
        I = Ins(eng, fn, kind, dsem, inc)
        deps = {}

        def dep(J, hazard):
            if J is None or J is I:
                return
            if J.kind == "c" and J.eng == eng and kind == "c":
                if eng == "pe":
                    return
                if not SAME_ENGINE_SYNC:
                    return
            deps[id(J)] = J

        for k in reads:
            dep(self.last_w.get(k), "raw")
        for k in writes:
            dep(self.last_w.get(k), "waw")
            for J in self.readers.get(k, {}).values():
                dep(J, "war")
        I.deps = [(J, ((self.dsem_count[J.dsem] if DSEM_TOTAL else J.dcount) if J.kind == "d" else 0)) for J in deps.values()]
        for J, _ in I.deps:
            J.signals = True
        if kind == "d":
            self.dsem_count[dsem] = self.dsem_count.get(dsem, 0) + inc
            I.dcount = self.dsem_count[dsem]
        for k in reads:
            r = self.readers.setdefault(k, {})
            r[(I.kind, I.eng if I.kind == "c" else I.dsem)] = I
        for k in writes:
            self.last_w[k] = I
            self.readers[k] = {}
        self.ins.append(I)
        return I

    def emit(self, nc, sems, dsems, final_waits):
        cnt = {e: 0 for e in self.ENGS}
        for I in self.ins:
            if I.kind == "c" and I.signals:
                cnt[I.eng] += 1
                I.sig = cnt[I.eng]
        streams = {e: [I for I in self.ins if I.eng == e] for e in self.ENGS}
        nwaits = [0]
        waitlog = self.waitlog = []

        def run(engname, eng):
            known = {}
            for I in streams[engname]:
                need = {}
                for J, dval in I.deps:
                    if J.kind == "c":
                        key = ("e", J.eng)
                        val = J.sig
                    else:
                        key = ("d", J.dsem)
                        val = dval
                    if val > need.get(key, 0):
                        need[key] = val
                for key, val in need.items():
                    if known.get(key, 0) >= val:
                        continue
                    known[key] = val
                    s = sems[key[1]] if key[0] == "e" else dsems[key[1]]
                    eng.wait_ge(s, val)
                    nwaits[0] += 1
                    waitlog.append((engname, key, val))
                bi = I.fn(eng)
                if I.kind == "d":
                    bi.then_inc(dsems[I.dsem], I.inc)
                elif I.signals:
                    bi.then_inc(sems[I.eng], 1)
            if engname in final_waits:
                for dname in final_waits[engname]:
                    eng.wait_ge(dsems[dname], self.dsem_count[dname])

        with nc.Block() as block:
            @block.tensor
            def _(e):
                run("pe", e)

            @block.scalar
            def _(e):
                run("act", e)

            @block.vector
            def _(e):
                run("dve", e)

            @block.gpsimd
            def _(e):
                run("pool", e)

            @block.sync
            def _(e):
                run("sp", e)
        return nwaits[0]


def make_tables():
    ki = np.arange(128, dtype=np.float64)[:, None]
    qi = np.arange(TQ, dtype=np.float64)[None, :]
    tabB = np.zeros((4, 128, 4, 2, TQ), np.float64)
    biasL = np.zeros((4, 128, NBL), np.float64)
    biasR = np.zeros((4, 128, 16), np.float64)
    for h in range(4):
        s = SLOPE_B[h]
        for o in range(2):
            tabB[h, :, o, :, :] = np.exp(-s * np.abs(qi - 128 * o - ki))[:, None, :]
        tabB[h, :, 2, :, :] = (np.exp(-s * qi) + 0 * ki)[:, None, :]
        tabB[h, :, 3, :, :] = (np.exp(-s * (TQ - 1 - qi)) + 0 * ki)[:, None, :]
        for n in range(NBL):
            biasL[h, :, n] = -s * (128 * n - ki[:, 0])
        for o in range(16):
            biasR[h, :, o] = -s * (128 * o - (TQ - 1) + ki[:, 0])
    tabB = tabB.reshape(4, 128, 4 * 2 * TQ)
    bias = np.concatenate([biasL.transpose(1, 0, 2).reshape(128, 4 * NBL),
                           biasR.transpose(1, 0, 2).reshape(128, 4 * 16)], axis=1)
    q1 = np.arange(128, dtype=np.float64)[None, :]
    tabA = np.zeros((8, 128, 4, 128), np.float64)
    for h in range(8):
        s = SLOPE_A[h]
        for oi, o in enumerate((-1, 0, 1)):
            d = np.abs(q1 - (128 * o + ki))
            tabA[h, :, oi, :] = np.exp(-s * d) * (d <= 128)
        d = 255 - ki - q1
        tabA[h, :, 3, :] = np.exp(-s * d) * (d <= 128)
    tabA = tabA.transpose(1, 0, 2, 3).reshape(128, 8 * 512)
    return (tabB.astype(ml_dtypes.bfloat16), bias.astype(np.float32), tabA.astype(ml_dtypes.bfloat16))


def build_nc(n_layers=DEPTH, groups=None, dbg=None, stop=None):
    groups = groups or [[0, 1], [2, 3], [4, 5], [6, 7]]
    nc = bass.Bass("TRN2", target_bir_lowering=False)
    P = Prog()
    ctxs = []

    def dram(name, shape, dt, kind):
        return nc.dram_tensor(name, list(shape), dt, kind=kind).ap()

    xT_d = dram("xT", [D, NT], F32, "ExternalInput")
    w_d = {}
    for nm, shp in (("ffn1_w_gate", [DEPTH, D, DFF]), ("ffn1_w_up", [DEPTH, D, DFF]), ("ffn1_w_down", [DEPTH, DFF, D]),
                    ("w_in", [DEPTH, D, INC]), ("w_out", [DEPTH, D, D]),
                    ("ffn2_w_gate", [DEPTH, D, DFF]), ("ffn2_w_up", [DEPTH, D, DFF]), ("ffn2_w_down", [DEPTH, DFF, D])):
        w_d[nm] = dram(nm, shp, F32, "ExternalInput")
    gains_d = dram("gains", [128, 7 * 8], F32, "ExternalInput")
    sinkc_d = dram("sinkc", [128, DEPTH * 4], F32, "ExternalInput")
    lamv_d = dram("lamv", [128, DEPTH * 4 * 64], F32, "ExternalInput")
    subln_d = dram("subln", [128, DEPTH], F32, "ExternalInput")
    mask_d = dram("mask", [128, 2], F32, "ExternalInput")
    tabB_d = dram("tabB", [4, 128, 2048], BF16, "ExternalInput")
    bias_d = dram("biasT", [128, NBIAS], F32, "ExternalInput")
    tabA_d = dram("tabA", [128, 8 * 512], BF16, "ExternalInput")
    out_d = dram("outT", [D, NT], F32, "ExternalOutput")
    qa_d = dram("qa_s", [4, 128, NT], BF16, "Internal")
    qb_d = dram("qb_s", [4, 128, NT], BF16, "Internal")
    own_d = dram("own_s", [128, L_ROW], BF16, "Internal")
    send_d = dram("send_s", [2 * 128, L_ROW], BF16, "Internal")
    recv_d = dram("recv_s", [128, L_ROW], BF16, "Internal")
    dbg_out = {}
    if dbg:
        for nm, shp, dt in dbg:
            dbg_out[nm] = dram(nm, shp, dt, "ExternalOutput")

    def sb(name, shape, dt):
        c = nc.sbuf_tensor(name, list(shape), dt)
        t = c.__enter__()
        ctxs.append(c)
        return t

    def psum(name):
        c = nc.psum_tensor(name, [128, 512], F32)
        t = c.__enter__()
        ctxs.append(c)
        return t

    xT = sb("xT_sb", [128, 8, NT], F32)
    hT = sb("hT_sb", [128, 8, NT], BF16)
    wsl = [sb(f"wsl{i}", [128, 2048], BF16) for i in range(NSLOT)]
    sq = sb("sq_sb", [128, 8, TT], BF16)
    rstd = sb("rstd_sb", [128, TT], F32)
    sg = [sb(f"sg{i}", [128, TT], F32) for i in range(2)]
    actT = [sb(f"actT{i}", [128, 2, TT], BF16) for i in range(2)]
    stg = [sb(f"stg{i}", [128, TT], BF16) for i in range(4)]
    stgm = [sb(f"stgm{i}", [128, 2, TT], BF16) for i in range(2)]
    gains = sb("gains_sb", [128, 7 * 8], F32)
    sinkc = sb("sinkc_sb", [128, DEPTH * 4], F32)
    esink = sb("esink_sb", [128, DEPTH * 4], F32)
    lamv = sb("lamv_sb", [128, DEPTH * 4 * 64], F32)
    lamt = sb("lamt_sb", [128, 64], F32)
    lams = sb("lams_sb", [128, 8], F32)
    neglam = sb("neglam_sb", [128, DEPTH], F32)
    subln = sb("subln_sb", [128, DEPTH], F32)
    maskc = sb("mask_sb", [128, 2], F32)
    biasT = sb("bias_sb", [128, NBIAS], F32)
    tabA2 = [sb(f"tabA_sb{i}", [128, 2, 4, 128], BF16) for i in range(2)]
    tabB = [sb(f"tabB{i}", [128, 2048], BF16) for i in range(1)]
    ones1024 = sb("ones1024", [128, 128], BF16)
    ones128 = sb("ones128", [128, 128], BF16)
    ones1 = sb("ones1", [128, 128], BF16)
    epsc = sb("epsc", [128, 1], F32)
    QB = [sb(f"QB{i}", [128, NT], BF16) for i in range(2)]
    KB = [sb(f"KB{i}", [128, 2 * NT], BF16) for i in range(1)]
    VB = [sb(f"VB{i}", [128, 32, 128], BF16) for i in range(1)]
    QA = [sb(f"QA{i}", [128, NT], BF16) for i in range(2)]
    KA = [sb(f"KA{i}", [128, NT + 128], BF16) for i in range(1)]
    VA = [sb(f"VA{i}", [128, 17, 64], BF16) for i in range(1)]
    fin3 = sb("fin3", [128, TT], F32)
    ytmp = [sb(f"ytmp{i}", [128, TT], F32) for i in range(2)]
    Eb = [actT[0][:, 0, :], actT[0][:, 1, :], actT[1][:, 0, :], actT[1][:, 1, :]]
    Pb = [stgm[0][:, 0, :], stgm[0][:, 1, :], stgm[1][:, 0, :], stgm[1][:, 1, :]]
    fin = [sg[0][:, :], sg[1][:, :], rstd[:, :], fin3[:, :]]
    EK = lambda i: ("act", i // 2, i % 2)
    PK = lambda i: ("stgm", i // 2, i % 2)
    FK = lambda i: (("sg", 0), ("sg", 1), ("rstd",), ("fin3",))[i]
    ps = [psum(f"ps{i}") for i in range(8)]

    def K(*a):
        return tuple(a)

    def PSK(b):
        return [("ps", b, 0), ("ps", b, 1)]

    def dma(eng, out, in_, reads, writes, dsem):
        P.add(eng, lambda e, out=out, in_=in_: e.dma_start(out=out, in_=in_), reads, writes, kind="d", dsem=dsem)

    def mm(out, lhsT, rhs, start, stop, reads, writes):
        P.add("pe", lambda e: e.matmul(out, lhsT, rhs, start=start, stop=stop), reads, writes)

    wlist = []

    def wsrc_cols(name, l, c0):
        return w_d[name][l, :, c0:c0 + 256].rearrange("(k p) n -> p k n", p=128)

    def wsrc_rows(name, l, r0):
        return w_d[name][l, r0:r0 + 256, :].rearrange("(k p) n -> p k n", p=128)

    WIN_ORDER = [5, 6, 7, 8, 2, 0, 1, 3, 4]
    for l in range(n_layers):
        for s in range(NSL):
            wlist.append((wsrc_cols("ffn1_w_gate", l, s * 256), 8))
            wlist.append((wsrc_cols("ffn1_w_up", l, s * 256), 8))
            wlist.append((wsrc_rows("ffn1_w_down", l, s * 256), 2))
        for i in WIN_ORDER:
            wlist.append((wsrc_cols("w_in", l, i * 256), 8))
        for i in range(4):
            wlist.append((wsrc_cols("w_out", l, i * 256), 8))
        for s in range(NSL):
            wlist.append((wsrc_cols("ffn2_w_gate", l, s * 256), 8))
            wlist.append((wsrc_cols("ffn2_w_up", l, s * 256), 8))
            wlist.append((wsrc_rows("ffn2_w_down", l, s * 256), 2))
    wstate = {"next": 0}

    def wview(idx):
        slot = idx % NSLOT
        a = wlist[idx][1]
        return wsl[slot][:, :].rearrange("p (a b) -> p a b", a=a), K("w", slot)

    def wensure(upto):
        upto = min(upto, len(wlist) - 1)
        while wstate["next"] <= upto:
            idx = wstate["next"]
            v, key = wview(idx)
            dma("pool", v, wlist[idx][0], [], [key], f"w{idx % NSLOT}")
            wstate["next"] += 1

    LA = 2
    wcur = {"i": 0}

    def wtake(n):
        i0 = wcur["i"]
        wcur["i"] += n
        wensure(i0 + n - 1 + LA)
        return [wview(i0 + j) for j in range(n)]

    for c in range(8):
        dma("sp", xT[:, c, :], xT_d[c * 128:(c + 1) * 128, :], [], [K("x", c, t) for t in range(NTT)], "xin")
    for t_, d_, nm in ((gains, gains_d, "gains"), (sinkc, sinkc_d, "sinkc"), (lamv, lamv_d, "lamv"), (subln, subln_d, "subln"),
                       (maskc, mask_d, "mask"), (biasT, bias_d, "bias")):
        dma("sp", t_[:, :], d_, [], [K(nm)], "small")
    P.add("pool", lambda e: e.memset(ones1024[:, :], 1.0 / 1024), [], [K("ones1024")])
    P.add("pool", lambda e: e.memset(ones128[:, :], 1.0 / 128), [], [K("ones128")])
    P.add("pool", lambda e: e.memset(ones1[:, :], 1.0), [], [K("ones1")])
    P.add("pool", lambda e: e.memset(epsc[:, :], EPS), [], [K("epsc")])
    P.add("pool", lambda e: e.memset(QB[0][64:128, :], 0.0), [], [K("QBz", 0)])
    P.add("pool", lambda e: e.memset(QB[1][0:64, :], 0.0), [], [K("QBz", 1)])
    wensure(LA)
    P.add("act", lambda e: e.activation(out=esink[:, :], in_=sinkc[:, :], func=AF.Exp), [K("sinkc")], [K("esink")])
    for l in range(n_layers):
        lam_init = 0.8 - 0.6 * math.exp(-0.3 * l)
        for j in range(2):
            a = lamv[:, (l * 4 + 2 * j) * 64:(l * 4 + 2 * j + 1) * 64]
            b = lamv[:, (l * 4 + 2 * j + 1) * 64:(l * 4 + 2 * j + 2) * 64]
            P.add("dve", lambda e, a=a, b=b: e.tensor_tensor(out=lamt[:, :], in0=a, in1=b, op=ALU.mult), [K("lamv")], [K("lamt")])
            P.add("dve", lambda e, l=l, j=j: e.reduce_sum(out=lams[:, l * 4 + j:l * 4 + j + 1], in_=lamt[:, :], axis=mybir.AxisListType.X),
                  [K("lamt")], [K("lams", l, j)])
            P.add("act", lambda e, l=l, j=j: e.activation(out=lams[:, l * 4 + 2 + j:l * 4 + 3 + j], in_=lams[:, l * 4 + j:l * 4 + j + 1], func=AF.Exp),
                  [K("lams", l, j)], [K("lame", l, j)])
        P.add("dve", lambda e, l=l, li=lam_init: e.scalar_tensor_tensor(out=neglam[:, l:l + 1], in0=lams[:, l * 4 + 3:l * 4 + 4], scalar=-li,
                                                                        in1=lams[:, l * 4 + 2:l * 4 + 3], op0=ALU.add, op1=ALU.subtract),
              [K("lame", l, 0), K("lame", l, 1)], [K("neglam", l)])
        P.add("dve", lambda e, l=l, li=lam_init: e.tensor_scalar_mul(out=subln[:, l:l + 1], in0=subln[:, l:l + 1], scalar1=1.0 - li),
              [K("subln")], [K("subln")])

    PS_ST = 6

    def emit_norm(gcol, tiles=range(NTT), final=False):
        for t in tiles:
            ts = slice(t * TT, (t + 1) * TT)
            for c in range(8):
                P.add("act", lambda e, c=c, ts=ts: e.activation(out=sq[:, c, :], in_=xT[:, c, ts], func=AF.Square),
                      [K("x", c, t)], [K("sq", c)])
            for c in range(8):
                mm(ps[PS_ST][:, :], ones1024[:, :], sq[:, c, :], c == 0, c == 7, [K("sq", c), K("ones1024")], [*PSK(PS_ST)])
            P.add("act", lambda e: e.activation(out=rstd[:, :], in_=ps[PS_ST][:, :], func=AF.Sqrt, bias=epsc[:, 0:1], scale=1.0),
                  [*PSK(PS_ST), K("epsc")], [K("rstd")])
            P.add("dve", lambda e: e.reciprocal(out=rstd[:, :], in_=rstd[:, :]), [K("rstd")], [K("rstd")])
            for c in range(8):
                g = gains[:, gcol * 8 + c:gcol * 8 + c + 1]
                if not final:
                    P.add("dve", lambda e, c=c, ts=ts, g=g: e.scalar_tensor_tensor(out=hT[:, c, ts], in0=xT[:, c, ts], scalar=g, in1=rstd[:, :],
                                                                                   op0=ALU.mult, op1=ALU.mult),
                          [K("x", c, t), K("rstd"), K("gains")], [K("h", c, t)])
                else:
                    P.add("dve", lambda e, c=c, ts=ts, g=g: e.scalar_tensor_tensor(out=xT[:, c, ts], in0=xT[:, c, ts], scalar=g, in1=rstd[:, :],
                                                                                   op0=ALU.mult, op1=ALU.mult),
                          [K("x", c, t), K("rstd"), K("gains")], [K("x", c, t)])

    def emit_ffn():
        its = [(s, t) for s in range(NSL) for t in range(NTT)]
        wv = {}

        def GU(n):
            s, t = its[n]
            if t == 0:
                wv[s] = wtake(3)
            (wg, kg), (wu, ku), _ = wv[s]
            ts = slice(t * TT, (t + 1) * TT)
            for j in range(2):
                pg, pu = (n * 2 + j) % 2, 2 + (n * 2 + j) % 2
                for k in range(8):
                    mm(ps[pg][:, :], wg[:, k, j * 128:(j + 1) * 128], hT[:, k, ts], k == 0, k == 7, [kg, K("h", k, t)], [*PSK(pg)])
                for k in range(8):
                    mm(ps[pu][:, :], wu[:, k, j * 128:(j + 1) * 128], hT[:, k, ts], k == 0, k == 7, [ku, K("h", k, t)], [*PSK(pu)])
                sgb = (n * 2 + j) % 2
                P.add("act", lambda e, pg=pg, sgb=sgb: e.activation(out=sg[sgb][:, :], in_=ps[pg][:, :], func=AF.Silu), [*PSK(pg)], [K("sg", sgb)])
                P.add("dve", lambda e, pu=pu, sgb=sgb, n=n, j=j: e.tensor_tensor(out=actT[n % 2][:, j, :], in0=ps[pu][:, :], in1=sg[sgb][:, :], op=ALU.mult),
                      [*PSK(pu), K("sg", sgb)], [K("act", n % 2, j)])

        def DN(n):
            s, t = its[n]
            (wd, kd) = wv[s][2]
            ts = slice(t * TT, (t + 1) * TT)
            for m in range(8):
                py = (4, 5, 7, 6)[m % 4]
                for j in range(2):
                    mm(ps[py][:, :], wd[:, j, m * 128:(m + 1) * 128], actT[n % 2][:, j, :], j == 0, j == 1, [kd, K("act", n % 2, j)], [*PSK(py)])
                if m % 2 == 0:
                    P.add("dve", lambda e, m=m, ts=ts, py=py: e.scalar_tensor_tensor(out=xT[:, m, ts], in0=ps[py][:, :], scalar=0.5, in1=xT[:, m, ts],
                                                                                    op0=ALU.mult, op1=ALU.add),
                          [*PSK(py), K("x", m, t)], [K("x", m, t)])
                else:
                    yb = (m // 2) % 2
                    P.add("act", lambda e, py=py, yb=yb: e.mul(out=ytmp[yb][:, :], in_=ps[py][:, :], mul=0.5), [*PSK(py)], [K("ytmp", yb)])
                    P.add("pool", lambda e, m=m, ts=ts, yb=yb: e.tensor_tensor(out=xT[:, m, ts], in0=xT[:, m, ts], in1=ytmp[yb][:, :], op=ALU.add),
                          [K("ytmp", yb), K("x", m, t)], [K("x", m, t)])

        for n in range(len(its) + 1):
            if n < len(its):
                GU(n)
            if n >= 1:
                DN(n - 1)

    stg_i = {"i": 0, "m": 0}
    OWN_KB = lambda h: [K("own", "kb", h, t) for t in range(NTT)]
    OWN_VB = [K("own", "vb", tb) for tb in range(16)]
    OWN_KA = [K("own", "ka", t) for t in range(NTT)]
    OWN_VA = [K("own", "va", tb) for tb in range(16)]
    SEND_KEYS = [K("send", "kb", h, t, r) for h in range(4) for t in range(NTT) for r in range(2)] + \
                [K("send", "vb", tb, r) for tb in range(16) for r in range(2)] + \
                [K("send", "ka", t, r) for t in range(NTT) for r in range(2)] + \
                [K("send", "va", tb, r) for tb in range(16) for r in range(2)]

    def emit_proj(l):
        wv = {}
        order = list(WIN_ORDER)

        def need(i):
            if i not in wv:
                assert order.pop(0) == i
                wv[i] = wtake(1)[0]
            return wv[i]

        def fm_chunk(i, half, dests):
            w, kw = need(i)
            for t in range(NTT):
                ts = slice(t * TT, (t + 1) * TT)
                pb = (stg_i["i"]) % 4
                for k in range(8):
                    mm(ps[pb][:, :], w[:, k, half * 128:(half + 1) * 128], hT[:, k, ts], k == 0, k == 7, [kw, K("h", k, t)], [*PSK(pb)])
                si = stg_i["i"] % 4
                stg_i["i"] += 1
                P.add("act", lambda e, si=si, pb=pb: e.copy(out=stg[si][:, :], in_=ps[pb][:, :]), [*PSK(pb)], [K("stg", si)])
                for (dfn, masked, dkeyf) in dests:
                    dkey = dkeyf(t)
                    if not masked:
                        dma("sp", dfn(t), stg[si][:, :], [K("stg", si)], [dkey], f"st{si}")
                    else:
                        mi = stg_i["m"] % 2
                        stg_i["m"] += 1
                        for r in range(2):
                            P.add("dve", lambda e, si=si, mi=mi, r=r: e.tensor_scalar_mul(out=stgm[mi][:, r, :], in0=stg[si][:, :], scalar1=maskc[:, r:r + 1]),
                                  [K("stg", si), K("mask")], [K("stgm", mi, r)])
                            dma("sp", dfn(t, r), stgm[mi][:, r, :], [K("stgm", mi, r)], [dkey + (r,)], f"sm{mi}")

        def tm_block(i_list, col0, ncols, dests):
            for i in i_list:
                need(i)
            for tb in range(16):
                t = tb // 4
                tsl = slice(tb * 128, (tb + 1) * 128)
                pb = (stg_i["i"]) % 4
                for ii, i in enumerate(i_list):
                    for k in range(8):
                        w, kw = wv[i]
                        n = min(256, ncols - ii * 256)
                        mm(ps[pb][:, ii * 256:ii * 256 + n], hT[:, k, tsl], w[:, k, col0:col0 + n], k == 0, k == 7, [kw, K("h", k, t)], [*PSK(pb)])
                si = stg_i["i"] % 4
                stg_i["i"] += 1
                P.add("act", lambda e, si=si, pb=pb, ncols=ncols: e.copy(out=stg[si][:, 0:ncols], in_=ps[pb][:, 0:ncols]), [*PSK(pb)], [K("stg", si)])
                for (dfn, masked, dkeyf) in dests:
                    dkey = dkeyf(tb)
                    if not masked:
                        dma("sp", dfn(tb), stg[si][:, 0:ncols], [K("stg", si)], [dkey], f"st{si}")
                    else:
                        mi = stg_i["m"] % 2
                        stg_i["m"] += 1
                        for r in range(2):
                            P.add("dve", lambda e, si=si, mi=mi, r=r, ncols=ncols: e.tensor_scalar_mul(out=stgm[mi][:, r, 0:ncols], in0=stg[si][:, 0:ncols],
                                                                                                  scalar1=maskc[:, r:r + 1]),
                                  [K("stg", si), K("mask")], [K("stgm", mi, r)])
                            dma("sp", dfn(tb, r), stgm[mi][:, r, 0:ncols], [K("stgm", mi, r)], [dkey + (r,)], f"sm{mi}")

        def ownfm(off):
            return lambda t: own_d[:, off + t * TT: off + (t + 1) * TT]

        def sendfm(off):
            return lambda t, r: send_d[r * 128:(r + 1) * 128, off + t * TT: off + (t + 1) * TT]

        for h in range(4):
            i, half = 5 + h // 2, h % 2
            fm_chunk(i, half, [(ownfm(OFF_KB + h * NT), False, lambda t, h=h: K("own", "kb", h, t)),
                               (sendfm(OFF_KB + h * NT), True, lambda t, h=h: K("send", "kb", h, t))])
        tm_block([7, 8], 0, 512,
                 [(lambda tb: own_d[:, OFF_VB + tb * 512: OFF_VB + (tb + 1) * 512], False, lambda tb: K("own", "vb", tb)),
                  (lambda tb, r: send_d[r * 128:(r + 1) * 128, OFF_VB + tb * 512: OFF_VB + (tb + 1) * 512], True, lambda tb: K("send", "vb", tb))])
        fm_chunk(2, 0, [(ownfm(OFF_KA), False, lambda t: K("own", "ka", t)), (sendfm(OFF_KA), True, lambda t: K("send", "ka", t))])
        tm_block([2], 128, 128,
                 [(lambda tb: own_d[:, OFF_VA + tb * 128: OFF_VA + (tb + 1) * 128], False, lambda tb: K("own", "va", tb)),
                  (lambda tb, r: send_d[r * 128:(r + 1) * 128, OFF_VA + tb * 128: OFF_VA + (tb + 1) * 128], True, lambda tb: K("send", "va", tb))])
        P.add("pool", lambda e: e.collective_compute("ReduceScatter", ALU.add, replica_groups=groups, ins=[send_d], outs=[recv_d]),
              SEND_KEYS, [K("recv")], kind="d", dsem="cc", inc=1)
        for c in range(4):
            fm_chunk(c // 2, c % 2, [(lambda t, c=c: qa_d[c, :, t * TT:(t + 1) * TT], False, lambda t, c=c: K("qa", c, t))])
        for h in range(4):
            fm_chunk(3 + h // 2, h % 2, [(lambda t, h=h: qb_d[h, :, t * TT:(t + 1) * TT], False, lambda t, h=h: K("qb", h, t))])

    def emit_attn_a(l):
        LAG = 3
        for c in range(4):
            kv = c // 2
            cb = c % 2
            QAc, tabAc = QA[cb], tabA2[cb]
            dma("sp", QAc[:, :], qa_d[c, :, :], [K("qa", c, t) for t in range(NTT)], [K("QA", cb)], f"QA{cb}")
            dma("sp", tabAc[:, :, :, :], tabA_d[:, c * 1024:(c + 1) * 1024].rearrange("p (h o q) -> p h o q", h=2, o=4), [], [K("tabA", cb)], f"QA{cb}")
            if c % 2 == 0:
                for hh in range(2):
                    dma("sp", KA[0][hh * 64:(hh + 1) * 64, 0:NT], own_d[kv * 64:(kv + 1) * 64, OFF_KA:OFF_KA + NT], OWN_KA, [K("KA", hh)], "KA")
                    dma("pool", KA[0][hh * 64:(hh + 1) * 64, NT:NT + 128], recv_d[kv * 64:(kv + 1) * 64, OFF_KA + NT - 128:OFF_KA + NT], [K("recv")], [K("KAp", hh)], "KAp")
                dma("sp", VA[0][:, 0:16, :], own_d[:, OFF_VA:OFF_VA + 2048].rearrange("p (b c) -> p b c", c=128)[:, :, kv * 64:(kv + 1) * 64],
                    OWN_VA, [K("VA")], "VA")
                dma("pool", VA[0][:, 16, :], recv_d[:, OFF_VA + 15 * 128 + kv * 64: OFF_VA + 15 * 128 + (kv + 1) * 64], [K("recv")], [K("VAp")], "VAp")
            kreads = [K("KA", 0), K("KA", 1), K("KAp", 0), K("KAp", 1)]
            vreads = [K("VA"), K("VAp")]
            its = [(qb, hh) for qb in range(16) for hh in range(2)]

            def blocks_of(qb):
                blocks = []
                if qb >= 1:
                    blocks.append(((qb - 1) * 128, qb - 1, 0))
                blocks.append((qb * 128, qb, 1))
                if qb <= 14:
                    blocks.append(((qb + 1) * 128, qb + 1, 2))
                else:
                    blocks.append((NT, 16, 3))
                return blocks

            def S(n):
                qb, hh = its[n]
                qs = slice(qb * 128, (qb + 1) * 128)
                blocks = blocks_of(qb)
                nb = len(blocks)
                hp = slice(hh * 64, (hh + 1) * 64)
                pss = n % 4
                eb = n % 4
                for bi, (kc, vb, ti) in enumerate(blocks):
                    kr = [K("KAp", 0), K("KAp", 1)] if ti == 3 else [K("KA", 0), K("KA", 1)]
                    mm(ps[pss][:, bi * 128:(bi + 1) * 128], KA[0][hp, kc:kc + 128], QAc[hp, qs], True, True, kr + [K("QA", cb)], [*PSK(pss)])
                P.add("act", lambda e, eb=eb, pss=pss, nb=nb: e.activation(out=Eb[eb][:, 0:nb * 128], in_=ps[pss][:, 0:nb * 128], func=AF.Exp, scale=0.125),
                      [*PSK(pss)], [EK(eb)])
                t0 = blocks[0][2]
                if [b[2] for b in blocks] == list(range(t0, t0 + nb)):
                    P.add("dve", lambda e, eb=eb, hh=hh, t0=t0, nb=nb, tab=tabAc: e.tensor_tensor(out=Pb[eb][:, 0:nb * 128], in0=Eb[eb][:, 0:nb * 128],
                                                                                       in1=tab[:, hh, t0:t0 + nb, :].rearrange("p a b -> p (a b)"), op=ALU.mult),
                          [EK(eb), K("tabA", cb)], [PK(eb)])
                else:
                    for bi, (kc, vb, ti) in enumerate(blocks):
                        P.add("dve", lambda e, eb=eb, hh=hh, ti=ti, bi=bi, tab=tabAc: e.tensor_tensor(out=Pb[eb][:, bi * 128:(bi + 1) * 128], in0=Eb[eb][:, bi * 128:(bi + 1) * 128],
                                                                                         in1=tab[:, hh, ti, :], op=ALU.mult),
                              [EK(eb), K("tabA", cb)], [PK(eb)])

            def PV(n):
                qb, hh = its[n]
                t = qb // 4
                qs = slice(qb * 128, (qb + 1) * 128)
                blocks = blocks_of(qb)
                nb = len(blocks)
                hp = slice(hh * 64, (hh + 1) * 64)
                eb = n % 4
                po, pl = 4 + (qb % 2), 6 + (qb % 2)
                for bi, (kc, vb, ti) in enumerate(blocks):
                    vr = [K("VAp")] if ti == 3 else [K("VA")]
                    mm(ps[po][hp, 0:128], VA[0][:, vb, :], Pb[eb][:, bi * 128:(bi + 1) * 128], bi == 0, bi == nb - 1, vr + [PK(eb)], [K("ps", po, hh)])
                for bi, (kc, vb, ti) in enumerate(blocks):
                    mm(ps[pl][hp, 0:128], ones1[:, 0:64], Pb[eb][:, bi * 128:(bi + 1) * 128], bi == 0, bi == nb - 1, [K("ones1"), PK(eb)], [K("ps", pl, hh)])
                if hh == 1:
                    fb = qb % 2
                    P.add("dve", lambda e, fb=fb, pl=pl, c=c: e.tensor_scalar_add(out=fin[fb][:, 0:128], in0=ps[pl][:, 0:128], scalar1=esink[:, l * 4 + c:l * 4 + c + 1]),
                          [K("ps", pl, 0), K("ps", pl, 1), K("esink")], [FK(fb)])
                    P.add("dve", lambda e, fb=fb: e.reciprocal(out=fin[fb][:, 0:128], in_=fin[fb][:, 0:128]), [FK(fb)], [FK(fb)])
                    P.add("dve", lambda e, fb=fb, po=po, c=c, qs=qs: e.tensor_tensor(out=hT[:, c, qs], in0=ps[po][:, 0:128], in1=fin[fb][:, 0:128], op=ALU.mult),
                          [K("ps", po, 0), K("ps", po, 1), FK(fb)], [K("h", c, t)])

            N = len(its)
            for n in range(N + LAG):
                if n < N:
                    S(n)
                if n >= LAG:
                    PV(n - LAG)

    def emit_attn_b(l):
        LAG = 3
        NSB = 4
        for h in range(4):
            slope = SLOPE_B[h]
            hb = 0
            dma("sp", QB[0][0:64, :], qb_d[h, 0:64, :], [K("qb", h, t) for t in range(NTT)], [K("QB", 0)], "QB0")
            dma("sp", QB[1][64:128, :], qb_d[h, 64:128, :], [K("qb", h, t) for t in range(NTT)], [K("QB", 1)], "QB0")
            dma("sp", KB[hb][:, 0:NT], own_d[:, OFF_KB + h * NT: OFF_KB + (h + 1) * NT], OWN_KB(h), [K("KBo", hb)], f"KB{hb}")
            dma("pool", KB[hb][:, NT:2 * NT], recv_d[:, OFF_KB + h * NT: OFF_KB + (h + 1) * NT], [K("recv")], [K("KBp", hb)], "KBp")
            dma("sp", tabB[0][:, :], tabB_d[h, :, :], [], [K("tabB")], "tabB")
            dma("sp", VB[hb][:, 0:16, :], own_d[:, OFF_VB:OFF_VB + 8192].rearrange("p (b c) -> p b c", c=512)[:, :, h * 128:(h + 1) * 128],
                OWN_VB, [K("VBo", hb)], f"VB{hb}")
            dma("pool", VB[hb][:, 16:32, :], recv_d[:, OFF_VB:OFF_VB + 8192].rearrange("p (b c) -> p b c", c=512)[:, :, h * 128:(h + 1) * 128],
                [K("recv")], [K("VBp", hb)], "VBp")
            kreads = [K("KBo", hb), K("KBp", hb), K("QB", 0), K("QB", 1), K("QBz", 0), K("QBz", 1)]
            vreads = [K("VBo", hb), K("VBp", hb)]
            its = []
            for tq in range(NTQ):
                blocks = []
                for j in range(16):
                    o = j - 2 * tq
                    if 0 <= o <= 1:
                        blocks.append((j * 128, j, None, o * 512))
                    elif o < 0:
                        n = -o
                        if slope * (128 * n - 127) > SKIP_T:
                            continue
                        blocks.append((j * 128, j, h * NBL + n, 1024))
                    else:
                        if slope * (128 * o - (TQ - 1)) > SKIP_T:
                            continue
                        blocks.append((j * 128, j, 4 * NBL + h * 16 + o, 1536))
                for j in range(16):
                    v = 30 - 2 * tq - j
                    if slope * (128 * v - 127) > SKIP_T:
                        continue
                    blocks.append((NT + j * 128, 16 + j, h * NBL + v, 1536))
                nb = len(blocks)
                for bi, (kc, vb, bcol, tcol) in enumerate(blocks):
                    its.append((tq, kc, vb, bcol, tcol, bi == 0, bi == nb - 1))

            def S(n):
                tq, kc, vb, bcol, tcol, first, last = its[n]
                qs = slice(tq * TQ, (tq + 1) * TQ)
                pss = n % NSB
                eb = n % 4
                for ty in range(2):
                    kr = [K("KBp", hb) if kc >= NT else K("KBo", hb), K("QB", 0), K("QB", 1), K("QBz", 0), K("QBz", 1)]
                    mm(ps[pss][:, ty * TQ:(ty + 1) * TQ], KB[hb][:, kc:kc + 128], QB[ty][:, qs], True, True, kr, [*PSK(pss)])
                if bcol is None:
                    P.add("act", lambda e, eb=eb, pss=pss: e.activation(out=Eb[eb][:, :], in_=ps[pss][:, :], func=AF.Exp, scale=0.125),
                          [*PSK(pss)], [EK(eb)])
                else:
                    P.add("act", lambda e, eb=eb, pss=pss, bcol=bcol: e.activation(out=Eb[eb][:, :], in_=ps[pss][:, :], func=AF.Exp, scale=0.125,
                                                                                   bias=biasT[:, bcol:bcol + 1]),
                          [*PSK(pss), K("bias")], [EK(eb)])
                P.add("dve", lambda e, eb=eb, tcol=tcol: e.tensor_tensor(out=Pb[eb][:, :], in0=Eb[eb][:, :], in1=tabB[0][:, tcol:tcol + 512], op=ALU.mult),
                      [EK(eb), K("tabB")], [PK(eb)])

            pending = []

            def stage_b(tq):
                po = 4 + 2 * (tq % 2)
                P.add("dve", lambda e, po=po: e.tensor_tensor(out=fin[0][:, :], in0=ps[po][:, :], in1=fin[0][:, :], op=ALU.mult), [*PSK(po), FK(0)], [FK(0)])
                P.add("dve", lambda e: e.scalar_tensor_tensor(out=fin[2][:, 0:TQ], in0=fin[0][:, TQ:2 * TQ], scalar=neglam[:, l:l + 1], in1=fin[0][:, 0:TQ],
                                                              op0=ALU.mult, op1=ALU.add),
                      [FK(0), K("neglam", l)], [FK(2)])
                P.add("pool", lambda e: e.tensor_tensor(out=sq[:, 0, 0:TQ], in0=fin[2][:, 0:TQ], in1=fin[2][:, 0:TQ], op=ALU.mult), [FK(2)], [K("sq", 0)])

            def stage_c(tq):
                t = tq // 2
                qs = slice(tq * TQ, (tq + 1) * TQ)
                pl = 5 + 2 * (tq % 2)
                mm(ps[pl][:, 0:TQ], ones128[:, :], sq[:, 0, 0:TQ], True, True, [K("sq", 0), K("ones128")], [*PSK(pl)])
                P.add("act", lambda e, pl=pl: e.activation(out=fin[3][:, 0:TQ], in_=ps[pl][:, 0:TQ], func=AF.Ln, bias=epsc[:, 0:1], scale=1.0),
                      [*PSK(pl), K("epsc")], [FK(3)])
                P.add("act", lambda e: e.activation(out=fin[3][:, 0:TQ], in_=fin[3][:, 0:TQ], func=AF.Exp, scale=-0.5), [FK(3)], [FK(3)])
                P.add("dve", lambda e, h=h, qs=qs: e.scalar_tensor_tensor(out=hT[:, 4 + h, qs], in0=fin[2][:, 0:TQ], scalar=subln[:, l:l + 1], in1=fin[3][:, 0:TQ],
                                                                          op0=ALU.mult, op1=ALU.mult),
                      [FK(2), FK(3), K("subln")], [K("h", 4 + h, t)])

            def run_pending(upto):
                while pending and pending[0][0] <= upto:
                    due, fn, tq = pending.pop(0)
                    fn(tq)
                    if fn is stage_b:
                        pending.append((due + 3, stage_c, tq))

            def PV(n):
                tq, kc, vb, bcol, tcol, first, last = its[n]
                eb = n % 4
                po, pl = 4 + 2 * (tq % 2), 5 + 2 * (tq % 2)
                vr = [K("VBp", hb) if vb >= 16 else K("VBo", hb)]
                mm(ps[po][:, :], VB[hb][:, vb, :], Pb[eb][:, :], first, last, vr + [PK(eb)], [*PSK(po)])
                mm(ps[pl][:, :], ones1[:, :], Pb[eb][:, :], first, last, [K("ones1"), PK(eb)], [*PSK(pl)])
                if last:
                    run_pending(10 ** 9)
                    P.add("act", lambda e, pl=pl: e.activation(out=fin[0][:, :], in_=ps[pl][:, :], func=AF.Ln), [*PSK(pl)], [FK(0)])
                    P.add("act", lambda e: e.activation(out=fin[0][:, :], in_=fin[0][:, :], func=AF.Exp, scale=-1.0), [FK(0)], [FK(0)])
                    pending.append((n + 1, stage_b, tq))

            N = len(its)
            for n in range(N + LAG):
                if n < N:
                    S(n)
                if n >= LAG:
                    PV(n - LAG)
                    run_pending(n - LAG)
            run_pending(10 ** 9)
            run_pending(10 ** 9)

    def emit_wout(next_norm=None):
        wv = wtake(4)
        for i in range(4):
            w, kw = wv[i]
            for t in range(NTT):
                ts = slice(t * TT, (t + 1) * TT)
                for mh in range(2):
                    m = 2 * i + mh
                    py = (t * 2 + mh) % 4
                    for k in range(8):
                        mm(ps[py][:, :], w[:, k, mh * 128:(mh + 1) * 128], hT[:, k, ts], k == 0, k == 7, [kw, K("h", k, t)], [*PSK(py)])
                    P.add("dve", lambda e, m=m, ts=ts, py=py: e.tensor_tensor(out=xT[:, m, ts], in0=ps[py][:, :], in1=xT[:, m, ts], op=ALU.add),
                          [*PSK(py), K("x", m, t)], [K("x", m, t)])
                if next_norm is not None and i == 3:
                    emit_norm(next_norm, tiles=[t])

    def forward():
        for l in range(n_layers):
            emit_norm(0 + l)
            emit_ffn()
            if stop == ("ffn1", l):
                return
            emit_norm(2 + l)
            emit_proj(l)
            emit_attn_a(l)
            emit_attn_b(l)
            if stop == ("attn", l):
                return
            emit_wout(next_norm=4 + l)
            if stop == ("wout", l):
                return
            emit_ffn()
        emit_norm(6, final=True)

    forward()
    for t in range(NTT):
        for c in range(8):
            dma("sp", out_d[c * 128:(c + 1) * 128, t * TT:(t + 1) * TT], xT[:, c, t * TT:(t + 1) * TT], [K("x", c, t)], [K("out", c, t)], "out")
    if dbg:
        for nm, shp, dt in dbg:
            if nm == "dbg_qa":
                dma("sp", dbg_out[nm], qa_d, [K("qa", c, t) for c in range(4) for t in range(NTT)], [K("dbgo1")], "out")
            if nm == "dbg_qb":
                dma("sp", dbg_out[nm], qb_d, [K("qb", c, t) for c in range(4) for t in range(NTT)], [K("dbgo2")], "out")
            if nm == "dbg_own":
                dma("sp", dbg_out[nm], own_d, [kk for h in range(4) for kk in OWN_KB(h)] + OWN_VB + OWN_KA + OWN_VA, [K("dbgo3")], "out")
            if nm == "dbg_recv":
                dma("sp", dbg_out[nm], recv_d, [K("recv")], [K("dbgo4")], "out")
            if nm == "dbg_h":
                dma("sp", dbg_out[nm].rearrange("(c p) n -> p c n", p=128), hT[:, :, :], [K("h", c, t) for c in range(8) for t in range(NTT)], [K("dbgo")], "out")

    dnames = sorted(P.dsem_count.keys())
    sem_ctx = {}
    for nme in list(Prog.ENGS[:4]) + ["d_" + d for d in dnames]:
        c = nc.semaphore(nme)
        sem_ctx[nme] = c.__enter__()
        ctxs.append(c)
    sems = {e: sem_ctx[e] for e in Prog.ENGS[:4]}
    dsems = {d: sem_ctx["d_" + d] for d in dnames}
    nw = P.emit(nc, sems, dsems, {"sp": ["out"]})
    for c in reversed(ctxs):
        c.__exit__(None, None, None)
    nc._n_ins = len(P.ins)
    nc._n_waits = nw
    nc._waitlog = P.waitlog
    return nc


_TABLES = None


def make_in_maps(inputs):
    global _TABLES
    if _TABLES is None:
        _TABLES = make_tables()
    tabB, bias, tabA = _TABLES
    f = lambda a: np.ascontiguousarray(np.asarray(a, dtype=np.float32))
    x = f(inputs["x"])
    gl = []
    for nm in ("ffn1_norm", "mix_norm", "ffn2_norm"):
        g = f(inputs[nm])
        for l in range(DEPTH):
            gl.append(g[l].reshape(8, 128).T)
    gl.append(f(inputs["final_norm"]).reshape(8, 128).T)
    gains = np.ascontiguousarray(np.concatenate(gl, axis=1))
    sink = f(inputs["sink"])
    sinkc = np.zeros((128, DEPTH * 4), np.float32)
    for l in range(DEPTH):
        for c in range(4):
            sinkc[0:64, l * 4 + c] = sink[l, 2 * c]
            sinkc[64:128, l * 4 + c] = sink[l, 2 * c + 1]
    lamv = np.concatenate([np.broadcast_to(f(inputs[nm])[l][None, :], (128, 64)) for l in range(DEPTH)
                           for nm in ("lam_q1", "lam_k1", "lam_q2", "lam_k2")], axis=1)
    lamv = np.ascontiguousarray(lamv)
    subln = np.ascontiguousarray(f(inputs["diff_subln"]).T)
    wts = {nm: f(inputs[nm]) for nm in ("ffn1_w_gate", "ffn1_w_up", "ffn1_w_down", "w_in", "w_out",
                                        "ffn2_w_gate", "ffn2_w_up", "ffn2_w_down")}
    maps = []
    for c in range(8):
        b, r = c // 2, c % 2
        xs = x[b, r * NT:(r + 1) * NT, :]
        if r == 1:
            xs = xs[::-1]
        m = dict(wts)
        m["xT"] = np.ascontiguousarray(xs.T)
        m["gains"] = gains
        m["sinkc"] = sinkc
        m["lamv"] = lamv
        m["subln"] = subln
        mk = np.zeros((128, 2), np.float32)
        mk[:, 1 - r] = 1.0
        m["mask"] = mk
        m["tabB"] = tabB
        m["biasT"] = bias
        m["tabA"] = tabA
        maps.append(m)
    return maps


def assemble(results):
    out = np.zeros((4, 2 * NT, D), np.float32)
    for c in range(8):
        b, r = c // 2, c % 2
        o = np.asarray(results[c]["outT"]).T
        if r == 1:
            o = o[::-1]
        out[b, r * NT:(r + 1) * NT, :] = o
    return out


_NC = None


def kernel(**inputs):
    global _NC
    if _NC is None:
        _NC = build_nc()
    maps = make_in_maps(inputs)
    res = run_bass_kernel_spmd(_NC, maps, core_ids=list(range(8)))
    return assemble(res.results)
```
